# Optimizing a Trainium2 kernel written in Bass

```python
import jax, jax.numpy as jnp
from jax import lax
import numpy as np

D_MODEL = 2048
BATCH = 16
SEQ = 2048
DEPTH = 4

GRID_W = 64
CTX_LEN = 256
N_MIXERS = 3
CONV_WIDTH = 31
FNET_GROUPS = 8
FNET_GROUP_W = D_MODEL // FNET_GROUPS
NA_HEADS = 16
NA_HEAD_DIM = D_MODEL // NA_HEADS
WIN_H = 8
WIN_W = 16
D_FF = 5632
FFN_CONV_WIDTH = 3
EPS = 1e-6
N_A = (DEPTH + 2) // 3
N_B = (DEPTH + 1) // 3
N_C = DEPTH // 3

kernel_name = "hybrid_conformer_fnet_natten_dit"


def rms_norm(t, g):
    tf = t.astype(jnp.float32)
    y = tf * lax.rsqrt(jnp.mean(tf * tf, axis=-1, keepdims=True) + EPS)
    return (y * g.astype(jnp.float32)).astype(t.dtype)


def layer_norm(t, g, b):
    tf = t.astype(jnp.float32)
    mu = jnp.mean(tf, axis=-1, keepdims=True)
    var = jnp.mean(jnp.square(tf - mu), axis=-1, keepdims=True)
    y = (tf - mu) * lax.rsqrt(var + EPS)
    return (y * g.astype(jnp.float32) + b.astype(jnp.float32)).astype(t.dtype)


def modulate(t, shift, scale):
    return t * (1.0 + scale) + shift


def depthwise_conv(t, w, b):
    k = w.shape[0]
    y = lax.conv_general_dilated(t, w[:, None, :], window_strides=(1,), padding=[(k // 2, k // 2)],
                                 dimension_numbers=("NWC", "WIO", "NWC"), feature_group_count=t.shape[-1])
    return y + b


def conformer_conv(h, w_pw1, b_pw1, w_dw, b_dw, ln_g, ln_b, w_pw2, b_pw2):
    u = h @ w_pw1 + b_pw1
    a, g = jnp.split(u, 2, axis=-1)
    u = a * jax.nn.sigmoid(g)
    u = depthwise_conv(u, w_dw, b_dw)
    u = jax.nn.silu(layer_norm(u, ln_g, ln_b))
    return u @ w_pw2 + b_pw2


def fourier_mix(h, w_out, b_out):
    bsz, length, d = h.shape
    hg = h.astype(jnp.float32).reshape(bsz, length, FNET_GROUPS, FNET_GROUP_W)
    f = jnp.fft.fftn(hg, axes=(1, 3), norm="ortho").real
    return f.reshape(bsz, length, d).astype(h.dtype) @ w_out + b_out


def conv_ffn(h, w_up, w_dw, b_dw, w_down):
    u = depthwise_conv(h @ w_up, w_dw, b_dw)
    v, g = jnp.split(u, 2, axis=-1)
    return (jax.nn.silu(g) * v) @ w_down


def neighbourhood_attention(h, hc, w_qkv, q_g, k_g, rpb, w_o, ctx_out):
    bsz, length, d = h.shape
    rows = length // GRID_W
    kh = min(WIN_H, rows)
    n_loc = kh * GRID_W

    def proj(t):
        qkv = (t @ w_qkv).reshape(t.shape[0], t.shape[1], 3, NA_HEADS, NA_HEAD_DIM)
        q = rms_norm(qkv[:, :, 0], q_g) * (NA_HEAD_DIM ** -0.5)
        k = rms_norm(qkv[:, :, 1], k_g)
        return q, k, qkv[:, :, 2]

    q, k, v = proj(h)
    qc, kc, vc = proj(hc)

    q_rows = q.reshape(bsz, rows, GRID_W, NA_HEADS, NA_HEAD_DIM).transpose(1, 0, 2, 3, 4)
    k_grid = k.reshape(bsz, rows, GRID_W, NA_HEADS, NA_HEAD_DIM)
    v_grid = v.reshape(bsz, rows, GRID_W, NA_HEADS, NA_HEAD_DIM)

    row_start = jnp.asarray(np.clip(np.arange(rows) - kh // 2, 0, rows - kh), jnp.int32)
    r_idx = jnp.arange(rows, dtype=jnp.int32)
    qcol = np.arange(GRID_W)[:, None]
    kcol = np.arange(GRID_W)[None, :]
    cstart = np.clip(qcol - WIN_W // 2, 0, GRID_W - WIN_W)
    col_mask = (kcol >= cstart) & (kcol < cstart + WIN_W)
    blk_mask = jnp.asarray(np.broadcast_to(col_mask[:, None, :], (GRID_W, kh, GRID_W)).reshape(GRID_W, n_loc))
    dc_idx = np.clip(kcol - qcol, -(WIN_W - 1), WIN_W - 1) + (WIN_W - 1)
    rpb_cols = rpb[:, :, dc_idx]

    def row_block(args):
        q_r, rs, r = args
        k_blk = lax.dynamic_slice_in_dim(k_grid, rs, kh, axis=1).reshape(bsz, n_loc, NA_HEADS, NA_HEAD_DIM)
        v_blk = lax.dynamic_slice_in_dim(v_grid, rs, kh, axis=1).reshape(bsz, n_loc, NA_HEADS, NA_HEAD_DIM)
        dr_idx = rs + jnp.arange(kh, dtype=jnp.int32) - r + (WIN_H - 1)
        bias = rpb_cols[:, dr_idx].transpose(0, 2, 1, 3).reshape(NA_HEADS, GRID_W, n_loc)
        s_loc = jnp.einsum("bqhd,bkhd->bhqk", q_r, k_blk).astype(jnp.float32) + bias.astype(jnp.float32)
        s_loc = jnp.where(blk_mask, s_loc, -jnp.inf)
        s_ctx = jnp.einsum("bqhd,bkhd->bhqk", q_r, kc).astype(jnp.float32)
        p = jax.nn.softmax(jnp.concatenate([s_loc, s_ctx], axis=-1), axis=-1).astype(v.dtype)
        return (jnp.einsum("bhqk,bkhd->bqhd", p[..., :n_loc], v_blk)
                + jnp.einsum("bhqk,bkhd->bqhd", p[..., n_loc:], vc))

    o = lax.map(row_block, (q_rows, row_start, r_idx))
    o = o.transpose(1, 0, 2, 3, 4).reshape(bsz, length, d) @ w_o
    oc = None
    if ctx_out:
        sc = jnp.einsum("bqhd,bkhd->bhqk", qc, kc).astype(jnp.float32)
        pc = jax.nn.softmax(sc, axis=-1).astype(vc.dtype)
        oc = jnp.einsum("bhqk,bkhd->bqhd", pc, vc).reshape(hc.shape[0], hc.shape[1], d) @ w_o
    return o, oc


def setup_inputs(seed: int = 0) -> dict:
    key = jax.random.key(seed)
    ks = jax.random.split(key, 32)
    D, F = D_MODEL, D_FF

    def nrm(k, shape, scale):
        return jax.random.normal(k, shape, jnp.float32) * scale

    def gain(k, shape):
        return 1.0 + nrm(k, shape, 0.02)

    return {
        "x": nrm(ks[0], (BATCH, SEQ, D), 1.0),
        "c": nrm(ks[1], (BATCH, D), 1.0),
        "ctx": nrm(ks[2], (BATCH, CTX_LEN, D), 1.0),
        "c_ctx": nrm(ks[3], (D,), 1.0),
        "mod_w": nrm(ks[4], (DEPTH, D, 6 * D), 0.5 * D ** -0.5),
        "mod_b": nrm(ks[5], (DEPTH, 6 * D), 0.02),
        "norm1_g": gain(ks[6], (DEPTH, D)),
        "norm2_g": gain(ks[7], (DEPTH, D)),
        "a_w_pw1": nrm(ks[8], (N_A, D, 2 * D), D ** -0.5),
        "a_b_pw1": nrm(ks[9], (N_A, 2 * D), 0.02),
        "a_w_dw": nrm(ks[10], (N_A, CONV_WIDTH, D), CONV_WIDTH ** -0.5),
        "a_b_dw": nrm(ks[11], (N_A, D), 0.02),
        "a_ln_g": gain(ks[12], (N_A, D)),
        "a_ln_b": nrm(ks[13], (N_A, D), 0.02),
        "a_w_pw2": nrm(ks[14], (N_A, D, D), D ** -0.5),
        "a_b_pw2": nrm(ks[15], (N_A, D), 0.02),
        "b_w_out": nrm(ks[16], (N_B, D, D), D ** -0.5),
        "b_b_out": nrm(ks[17], (N_B, D), 0.02),
        "c_w_qkv": nrm(ks[18], (N_C, D, 3 * D), D ** -0.5),
        "c_q_g": gain(ks[19], (N_C, NA_HEAD_DIM)),
        "c_k_g": gain(ks[20], (N_C, NA_HEAD_DIM)),
        "c_rpb": nrm(ks[21], (N_C, NA_HEADS, 2 * WIN_H - 1, 2 * WIN_W - 1), 0.1),
        "c_w_o": nrm(ks[22], (N_C, D, D), D ** -0.5),
        "f_w_up": nrm(ks[23], (DEPTH, D, 2 * F), D ** -0.5),
        "f_w_dw": nrm(ks[24], (DEPTH, FFN_CONV_WIDTH, 2 * F), FFN_CONV_WIDTH ** -0.5),
        "f_b_dw": nrm(ks[25], (DEPTH, 2 * F), 0.02),
        "f_w_down": nrm(ks[26], (DEPTH, F, D), F ** -0.5),
    }


def reference(x, c, ctx, c_ctx, mod_w, mod_b, norm1_g, norm2_g,
              a_w_pw1, a_b_pw1, a_w_dw, a_b_dw, a_ln_g, a_ln_b, a_w_pw2, a_b_pw2,
              b_w_out, b_b_out, c_w_qkv, c_q_g, c_k_g, c_rpb, c_w_o,
              f_w_up, f_w_dw, f_b_dw, f_w_down):
    silu_c = jax.nn.silu(c)
    silu_cc = jax.nn.silu(c_ctx)
    for i in range(DEPTH):
        kind = i % N_MIXERS
        j = i // N_MIXERS
        ctx_out = any(l % N_MIXERS == 2 for l in range(i + 1, DEPTH))
        ctx_in = ctx_out or kind == 2

        mod = silu_c @ mod_w[i] + mod_b[i]
        sh1, sc1, g1, sh2, sc2, g2 = jnp.split(mod[:, None, :], 6, axis=-1)
        h = modulate(rms_norm(x, norm1_g[i]), sh1, sc1)
        hc = None
        if ctx_in:
            modc = silu_cc @ mod_w[i] + mod_b[i]
            shc1, scc1, gc1, shc2, scc2, gc2 = jnp.split(modc, 6, axis=-1)
            hc = modulate(rms_norm(ctx, norm1_g[i]), shc1, scc1)

        yc = None
        if kind == 0:
            cp = (a_w_pw1[j], a_b_pw1[j], a_w_dw[j], a_b_dw[j], a_ln_g[j], a_ln_b[j], a_w_pw2[j], a_b_pw2[j])
            y = conformer_conv(h, *cp)
            if ctx_out:
                yc = conformer_conv(hc, *cp)
        elif kind == 1:
            y = fourier_mix(h, b_w_out[j], b_b_out[j])
            if ctx_out:
                yc = fourier_mix(hc, b_w_out[j], b_b_out[j])
        else:
            y, yc = neighbourhood_attention(h, hc, c_w_qkv[j], c_q_g[j], c_k_g[j], c_rpb[j], c_w_o[j], ctx_out)

        fp = (f_w_up[i], f_w_dw[i], f_b_dw[i], f_w_down[i])
        x = x + g1 * y
        x = x + g2 * conv_ffn(modulate(rms_norm(x, norm2_g[i]), sh2, sc2), *fp)
        if ctx_out:
            ctx = ctx + gc1 * yc
            ctx = ctx + gc2 * conv_ffn(modulate(rms_norm(ctx, norm2_g[i]), shc2, scc2), *fp)
    return x
```

```python
import bisect
from contextlib import ExitStack

import numpy as np
import ml_dtypes
import concourse.bass as bass
import concourse.mybir as mybir
from concourse.bass_utils import run_bass_kernel_spmd

F32 = mybir.dt.float32
BF16 = mybir.dt.bfloat16
AF = mybir.ActivationFunctionType
ALU = mybir.AluOpType

D = 2048
L = 2048
CL = 256
NB = 2
NCORES = 8
DFF = 5632
DEPTH = 4
EPS = 1e-6
GRID_W = 64
HD = 128
NH = 16
NEG = -30000.0


class Res:
    __slots__ = ("name", "w", "r", "sem")

    def __init__(self, name):
        self.name = name
        self.w = {}
        self.r = {}
        self.sem = None


class Sem:
    __slots__ = ("h", "count", "name")

    def __init__(self, h, name):
        self.h = h
        self.count = 0
        self.name = name


class Eng:
    def __init__(self, name, h, sem):
        self.name = name
        self.h = h
        self.sem = sem
        self.seq = 0
        self.sigseq = []
        self.waited = {}


class T:
    __slots__ = ("t", "res", "kres")

    def __init__(self, t, res):
        self.t = t
        self.res = res
        self.kres = None


class FW:
    N_DMA_SEMS = 92

    def __init__(self, nc, stack):
        self.nc = nc
        self.stack = stack
        mk = lambda n: Sem(stack.enter_context(nc.semaphore(n)), n)
        self.pe = Eng("pe", nc.tensor, mk("s_pe"))
        self.act = Eng("act", nc.scalar, mk("s_act"))
        self.dve = Eng("dve", nc.vector, mk("s_dve"))
        self.pool = Eng("pool", nc.gpsimd, mk("s_pool"))
        self.sp = Eng("sp", nc.sync, None)
        self.engs = [self.pe, self.act, self.dve, self.pool, self.sp]
        self.dsems = [mk(f"s_d{i}") for i in range(self.N_DMA_SEMS)]
        self.dsem_next = 0
        self.dsem_base = 0
        self.uid = 0
        self.all_events = {}
        self.n_waits = 0

    def res(self, name, dma=False):
        r = Res(name)
        if dma:
            assert self.dsem_next < self.N_DMA_SEMS, "out of DMA semaphores"
            r.sem = self.dsems[self.dsem_next]
            self.dsem_next += 1
        return r

    def sb(self, pstack, name, shape, dtype, dma=False):
        self.uid += 1
        t = pstack.enter_context(self.nc.sbuf_tensor(f"{name}_{self.uid}", shape, dtype))
        return T(t, self.res(name, dma))

    def _resolve(self, key, val):
        if isinstance(key, Eng):
            idx = bisect.bisect_left(key.sigseq, val)
            assert idx < len(key.sigseq), f"unsignaled dependency on {key.name} seq {val}"
            return key.sem, idx + 1
        return key, val

    def _wait(self, eng, deps):
        for key, val in deps.items():
            if key is eng and eng is self.pe:
                continue
            sem, v = self._resolve(key, val)
            if eng.waited.get(sem.name, 0) >= v:
                continue
            eng.h.wait_ge(sem.h, v)
            eng.waited[sem.name] = v
            self.n_waits += 1

    @staticmethod
    def _merge(dst, src):
        for k, v in src.items():
            if dst.get(k, 0) < v:
                dst[k] = v

    def _deps(self, r, w, pw):
        deps = {}
        for x in r:
            self._merge(deps, x.w)
        for x in w:
            self._merge(deps, x.w)
            self._merge(deps, x.r)
        for x in pw:
            self._merge(deps, x.r)
        return deps

    def _record(self, key, val, r, w, pw):
        for x in r:
            if x.r.get(key, 0) < val:
                x.r[key] = val
        for x in list(w) + list(pw):
            if x.w.get(key, 0) < val:
                x.w[key] = val
        if self.all_events.get(key, 0) < val:
            self.all_events[key] = val

    def op(self, eng, fn, r=(), w=(), pw=(), sig=True):
        self._wait(eng, self._deps(r, w, pw))
        ins = fn(eng.h)
        eng.seq += 1
        if sig:
            ins.then_inc(eng.sem.h, 1)
            eng.sigseq.append(eng.seq)
        self._record(eng, eng.seq, r, w, pw)
        return ins

    def dma(self, q, out, in_, semres, r=(), w=(), pw=()):
        self._wait(q, self._deps(r, w, pw))
        sem = semres.sem
        assert sem is not None, semres.name
        q.h.dma_start(out=out, in_=in_).then_inc(sem.h, 16)
        sem.count += 16
        self._record(sem, sem.count, r, w, pw)

    def barrier(self):
        for e in self.engs:
            if e is self.pool:
                continue
            self._wait(e, self.all_events)
        self.dsem_next = self.dsem_base

    def engine_sync(self, eng):
        self._wait(eng, self.all_events)

    def persist(self):
        self.dsem_base = self.dsem_next


class PV:
    def __init__(self):
        self.off = {}
        self.n = 0

    def add(self, name, ncols):
        self.off[name] = self.n
        self.n += ncols


def pv_layout():
    pv = PV()
    for i in range(DEPTH):
        pv.add(f"n1g{i}", 16)
        pv.add(f"n2g{i}", 16)
        pv.add(f"modb{i}", 96)
        pv.add(f"fwdw{i}", 3 * 88)
        pv.add(f"fbdw{i}", 88)
    for j in range(2):
        pv.add(f"abpw1{j}", 32)
        pv.add(f"awdw{j}", 31 * 16)
        pv.add(f"abdw{j}", 16)
        pv.add(f"alng{j}", 16)
        pv.add(f"alnb{j}", 16)
        pv.add(f"abpw2{j}", 16)
    pv.add("bbout", 16)
    pv.add("cqg", 1)
    pv.add("ckg", 1)
    return pv


PVL = pv_layout()

ATT_CHUNKS = {0: list(range(0, 6)), 1: list(range(2, 10)), 2: list(range(6, 14)), 3: list(range(10, 16))}
ATT_TILE0 = {0: 0, 1: 6, 2: 6, 3: 14}
N_BT = 20
WR_ELEMS = 8192
NWR = 2


class Prog:
    def __init__(self, nlayers=DEPTH, debug=(), upto=None):
        self.upto = upto
        self.nlayers = nlayers
        self.debug = set(debug)
        self.nc = bass.Bass("TRN2", target_bir_lowering=False)
        self.build()

    def din(self, name, shape, dtype=F32):
        return self.nc.dram_tensor(name, list(shape), dtype, kind="ExternalInput").ap()

    def dscr(self, name, shape, dtype):
        if name in self.debug:
            return self.nc.dram_tensor(name, list(shape), dtype, kind="ExternalOutput").ap()
        return self.nc.dram_tensor(name, list(shape), dtype).ap()

    def build(self):
        nc = self.nc
        self.x = self.din("x", [NB * L, D])
        self.ctx = self.din("ctx", [NB * CL, D])
        self.csil = self.din("csil", [128, 16, 3])
        self.pvec = self.din("pvec", [128, PVL.n])
        self.ident = self.din("ident", [128, 128])
        self.mod_w = self.din("mod_w", [DEPTH, D, 6 * D])
        self.a_w_pw1 = self.din("a_w_pw1", [2, D, 2 * D])
        self.a_w_pw2 = self.din("a_w_pw2", [2, D, D])
        self.b_w_out = self.din("b_w_out", [1, D, D])
        self.c_w_qkv = self.din("c_w_qkv", [1, D, 3 * D])
        self.c_w_o = self.din("c_w_o", [1, D, D])
        self.f_w_up = self.din("f_w_up", [DEPTH, D, 2 * DFF])
        self.f_w_down = self.din("f_w_down", [DEPTH, DFF, D])
        self.dft_cs = self.din("dft_cs", [2, 128, 2, 512], BF16)
        self.dft_l = self.din("dft_l", [2, 128, 16, L], BF16)
        self.dft_c = self.din("dft_c", [2, 128, 2, CL], BF16)
        self.att_bias = self.din("att_bias", [NH, 128, N_BT * 512])
        self.out = nc.dram_tensor("out", [NB * L, D], F32, kind="ExternalOutput").ap()
        self.XT = self.dscr("XT", [D, NB * L], F32)
        self.CT = self.dscr("CT", [D, NB * CL], F32)
        self.HT = self.dscr("HT", [D, NB * (L + CL)], BF16)
        self.HTC = self.dscr("HTC", [D, NB * CL], BF16)
        self.UT = self.dscr("UT", [D, L], F32)
        self.VT = self.dscr("VT", [D, L], F32)
        self.GT = self.dscr("GT", [DFF, L], BF16)
        self.AB = self.dscr("AB", [L, 8 * 512], BF16)
        self.YT = self.dscr("YT", [D, L], BF16)
        self.QT = self.dscr("QT", [D, L], BF16)
        self.KT = self.dscr("KT", [D, L + CL], BF16)
        self.VS = self.dscr("VS", [L + CL, D], BF16)
        self.OT = self.dscr("OT", [D, L], BF16)
        self.r_XT = [[Res(f"XT{b}_{c}") for c in range(16)] for b in range(NB)]
        _rct = [Res(f"CT_{c}") for c in range(16)]
        self.r_CT = [_rct for b in range(NB)]
        self.r_HTC = Res("HTC")
        self.r_HT = [Res(f"HT{b}") for b in range(NB)]
        self.r_scr = {n: Res(n) for n in ("UT", "VT", "GT", "AB", "YT", "QT", "KT", "VS", "OT")}

        with ExitStack() as gs:
            fw = self.fw = FW(nc, gs)
            self.ps = gs.enter_context(nc.psum_tensor("ps", [128, 4096], F32))
            self.bank = [fw.res(f"bank{i}") for i in range(8)]
            g_pv = fw.sb(gs, "pv", [128, PVL.n], F32, dma=True)
            g_id = fw.sb(gs, "ident", [128, 128], F32, dma=True)
            g_ones = fw.sb(gs, "ones", [128, 128], BF16)
            g_mod = fw.sb(gs, "mod", [128, DEPTH, 96, 3], F32)
            g_mA = fw.sb(gs, "modA", [128, DEPTH, 2, 16, 3], F32)
            self.g_pv, self.g_id, self.g_ones, self.g_mod, self.g_mA = g_pv, g_id, g_ones, g_mod, g_mA
            self.wring = [fw.sb(gs, "wring", [128, WR_ELEMS], BF16, dma=True) for _ in range(NWR)]
            self.wr_idx = 0
            fw.persist()
            fw.dma(fw.sp, g_pv.t[:], self.pvec, g_pv.res, w=[g_pv.res])
            fw.dma(fw.sp, g_id.t[:], self.ident, g_id.res, w=[g_id.res])
            fw.op(fw.pool, lambda e: e.memset(g_ones.t[:], 1.0), w=[g_ones.res])

            self.phase_mod()
            for b in range(NB):
                self.phase_transpose_in(self.x[b * L:(b + 1) * L, :], self.xt_view(b), L, self.r_XT[b])
                self.phase_transpose_in(self.ctx[b * CL:(b + 1) * CL, :], self.ct_view(b), CL, self.r_CT[b])
            for i in range(self.nlayers):
                self.layer(i)
            for b in range(NB):
                self.phase_transpose_out(b)
            fw.barrier()

    def xt_view(self, b):
        return self.XT.rearrange("(c p) t -> p c t", p=128)[:, :, b * L:(b + 1) * L]

    def ct_view(self, b):
        return self.CT.rearrange("(c p) t -> p c t", p=128)[:, :, b * CL:(b + 1) * CL]

    def ht_view(self, b, ctx):
        v = self.HT.rearrange("(c p) t -> p c t", p=128)
        o = b * (L + CL)
        return v[:, :, o + L:o + L + CL] if ctx else v[:, :, o:o + L]

    def pvc(self, name, col=0, n=1):
        o = PVL.off[name] + col
        return self.g_pv.t[:, o:o + n]

    def psw(self, b0, width):
        return self.ps[:, b0 * 512:b0 * 512 + width]

    def banks(self, b0, width):
        return self.bank[b0:b0 + (width + 511) // 512]

    def mm(self, out, lhsT, rhs, start, stop, r, w, sig=False):
        return self.fw.op(self.fw.pe, lambda e: e.matmul(out, lhsT, rhs, start=start, stop=stop), r=r, w=w, sig=sig)

    def tr(self, out, in_, r, w, sig=False):
        idt = self.g_id
        return self.fw.op(self.fw.pe, lambda e: e.transpose(out, in_, idt.t[:]), r=list(r) + [idt.res], w=w, sig=sig)

    def actf(self, out, in_, func, r, w=(), pw=(), bias=None, scale=None):
        kw = {}
        if bias is not None:
            kw["bias"] = bias
        if scale is not None:
            kw["scale"] = scale
        return self.fw.op(self.fw.act, lambda e: e.activation(out=out, in_=in_, func=func, **kw), r=r, w=w, pw=pw)

    def stt(self, out, in0, scalar, in1, op0, op1, r, w=(), pw=()):
        return self.fw.op(self.fw.dve, lambda e: e.scalar_tensor_tensor(out=out, in0=in0, scalar=scalar, in1=in1, op0=op0, op1=op1), r=r, w=w, pw=pw)

    def tt(self, out, in0, in1, op, r, w=(), pw=(), eng=None):
        return self.fw.op(eng or self.fw.dve, lambda e: e.tensor_tensor(out=out, in0=in0, in1=in1, op=op), r=r, w=w, pw=pw)

    def ts(self, out, in0, s1, s2, op0, op1, r, w=(), pw=(), eng=None):
        if s2 is None:
            return self.fw.op(eng or self.fw.dve, lambda e: e.tensor_scalar(out=out, in0=in0, scalar1=s1, scalar2=None, op0=op0), r=r, w=w, pw=pw)
        return self.fw.op(eng or self.fw.dve, lambda e: e.tensor_scalar(out=out, in0=in0, scalar1=s1, scalar2=s2, op0=op0, op1=op1), r=r, w=w, pw=pw)

    def cp(self, eng, out, in_, r, w=(), pw=()):
        fw = self.fw
        if eng is fw.act:
            return fw.op(eng, lambda e: e.activation(out=out, in_=in_, func=AF.Copy), r=r, w=w, pw=pw)
        return fw.op(eng, lambda e: e.tensor_copy(out, in_), r=r, w=w, pw=pw)

    def recip(self, out, in_, r, w=(), pw=()):
        return self.fw.op(self.fw.dve, lambda e: e.reciprocal(out, in_), r=r, w=w, pw=pw)

    def phase_mod(self):
        fw = self.fw
        g_mod, g_mA = self.g_mod, self.g_mA
        with ExitStack() as st:
            cs = fw.sb(st, "csil", [128, 16, 3], F32, dma=True)
            csb = fw.sb(st, "csilb", [128, 16, 3], BF16)
            fw.dma(fw.sp, cs.t[:], self.csil, cs.res, w=[cs.res])
            self.actf(csb.t[:], cs.t[:], AF.Silu, r=[cs.res], w=[csb.res])
            wb = [fw.sb(st, f"mw{k}", [128, 16, 512], BF16, dma=True) for k in range(3)]
            n = 0
            for i in range(self.nlayers):
                wv = self.mod_w[i].rearrange("(k p) n -> p k n", p=128)
                for piece in range(24):
                    buf = wb[n % 3]
                    bk = n % 2
                    n += 1
                    fw.dma(fw.pool, buf.t[:], wv[:, :, piece * 512:(piece + 1) * 512], buf.res, w=[buf.res])
                    for qq in range(4):
                        for k in range(16):
                            self.mm(self.ps[:, bk * 512 + qq * 3:bk * 512 + qq * 3 + 3],
                                    buf.t[:, k, qq * 128:(qq + 1) * 128], csb.t[:, k, :],
                                    start=(k == 0), stop=(k == 15), r=[buf.res, csb.res], w=[self.bank[bk]],
                                    sig=(k == 15 and qq == 3))
                    for qq in range(4):
                        q = piece * 4 + qq
                        self.ts(g_mod.t[:, i, q, :], self.ps[:, bk * 512 + qq * 3:bk * 512 + qq * 3 + 3],
                                self.pvc(f"modb{i}", q), None, ALU.add, None,
                                r=[self.bank[bk], self.g_pv.res], pw=[g_mod.res])
                for which in range(2):
                    for c in range(16):
                        sc = (16 if which == 0 else 64) + c
                        self.ts(g_mA.t[:, i, which, c, :], g_mod.t[:, i, sc, :], 1.0,
                                self.pvc(f"n{which + 1}g{i}", c), ALU.add, ALU.mult,
                                r=[g_mod.res, self.g_pv.res], pw=[g_mA.res])
            fw.barrier()

    def phase_transpose_in(self, src, dst, Lx, dres):
        fw = self.fw
        with ExitStack() as st:
            xin = [fw.sb(st, "xin", [128, 4, D], F32, dma=True) for _ in range(2)]
            stg = [fw.sb(st, "stg", [128, 16, 512], F32, dma=True) for _ in range(2)]
            it = 0
            nb = 0
            t0s = list(range(0, Lx, 512))

            def ld_x(k):
                t0 = t0s[k]
                nt = min(512, Lx - t0) // 128
                fw.dma(fw.sp, xin[k % 2].t[:, 0:nt, :], src[t0:t0 + nt * 128, :].rearrange("(n p) d -> p n d", p=128),
                       xin[k % 2].res, w=[xin[k % 2].res])

            ld_x(0)
            for t0 in t0s:
                nt = min(512, Lx - t0) // 128
                xi, sg = xin[it % 2], stg[it % 2]
                it += 1
                if it < len(t0s):
                    ld_x(it)
                for c in range(16):
                    bk = nb % 8
                    nb += 1
                    for n in range(nt):
                        self.tr(self.ps[:, bk * 512 + n * 128:bk * 512 + (n + 1) * 128],
                                xi.t[:, n, c * 128:(c + 1) * 128], r=[xi.res], w=[self.bank[bk]], sig=(n == nt - 1))
                    eng = fw.dve if c % 2 == 0 else fw.act
                    self.cp(eng, sg.t[:, c, 0:nt * 128], self.psw(bk, nt * 128), r=[self.bank[bk]], pw=[sg.res])
                fw.dma(fw.sp, dst[:, :, t0:t0 + nt * 128], sg.t[:, :, 0:nt * 128], sg.res, r=[sg.res], pw=dres)
            fw.barrier()

    def phase_transpose_out(self, b):
        fw = self.fw
        src = self.xt_view(b)
        with ExitStack() as st:
            xin = [fw.sb(st, "xin", [128, 16, 512], F32, dma=True) for _ in range(2)]
            stg = [fw.sb(st, "stg", [128, 4, D], F32, dma=True) for _ in range(2)]
            it = 0
            nb = 0
            def ld_o(k):
                fw.dma(fw.sp, xin[k % 2].t[:], src[:, :, k * 512:(k + 1) * 512], xin[k % 2].res, r=self.r_XT[b], w=[xin[k % 2].res])

            ld_o(0)
            for t0 in range(0, L, 512):
                xi, sg = xin[it % 2], stg[it % 2]
                it += 1
                if it < L // 512:
                    ld_o(it)
                for n in range(4):
                    for cq in range(4):
                        bk = nb % 8
                        nb += 1
                        for cc in range(4):
                            c = cq * 4 + cc
                            self.tr(self.ps[:, bk * 512 + cc * 128:bk * 512 + (cc + 1) * 128],
                                    xi.t[:, c, n * 128:(n + 1) * 128], r=[xi.res], w=[self.bank[bk]], sig=(cc == 3))
                        eng = fw.dve if nb % 2 == 0 else fw.act
                        self.cp(eng, sg.t[:, n, cq * 512:(cq + 1) * 512], self.psw(bk, 512), r=[self.bank[bk]], pw=[sg.res])
                fw.dma(fw.sp, self.out[b * L + t0:b * L + t0 + 512, :].rearrange("(n p) d -> p n d", p=128),
                       sg.t[:], sg.res, r=[sg.res])
            fw.barrier()

    def phase_norm_mod(self, jobs, i, which):
        fw = self.fw
        g_mA, g_mod = self.g_mA, self.g_mod
        shb = 0 if which == 0 else 48
        with ExitStack() as st:
            xts = [fw.sb(st, "xt", [128, 16, 512], F32, dma=True) for _ in range(3)]
            sq = fw.sb(st, "sq", [128, 16, 512], BF16)
            sds = [fw.sb(st, "sd", [128, 512], F32) for _ in range(2)]
            rss = [fw.sb(st, "rs", [128, 512], F32) for _ in range(2)]
            tmps = [fw.sb(st, "tmp", [128, 512], F32) for _ in range(2)]
            hbs = [fw.sb(st, "hb", [128, 16, 512], BF16, dma=True) for _ in range(2)]
            tiles = [(src, sres, dst, dres, rr, t0, min(512, Lx - t0))
                     for (src, sres, dst, dres, Lx, rr) in jobs for t0 in range(0, Lx, 512)]
            nt_ = len(tiles)
            cnt = [0]

            def ld(k):
                src, sres, dst, dres, rr, t0, n = tiles[k]
                xt = xts[k % 3]
                fw.dma(fw.sp, xt.t[:, :, 0:n], src[:, :, t0:t0 + n], xt.res, r=sres, w=[xt.res])

            def s1(k):
                n = tiles[k][6]
                xt = xts[k % 3]
                bk = k % 8
                self.actf(sq.t[:, :, 0:n], xt.t[:, :, 0:n], AF.Square, r=[xt.res], w=[sq.res])
                for c in range(16):
                    self.mm(self.psw(bk, n), self.g_ones.t[:], sq.t[:, c, 0:n], start=(c == 0), stop=(c == 15),
                            r=[sq.res, self.g_ones.res], w=[self.bank[bk]], sig=(c == 15))

            def s2(k):
                n = tiles[k][6]
                bk = k % 8
                sd, rs = sds[k % 2], rss[k % 2]
                self.actf(sd.t[:, 0:n], self.psw(bk, n), AF.Sqrt, r=[self.bank[bk]], w=[sd.res], bias=EPS, scale=1.0 / D)
                self.recip(rs.t[:, 0:n], sd.t[:, 0:n], r=[sd.res], w=[rs.res])

            def ap(k):
                src, sres, dst, dres, rr, t0, n = tiles[k]
                xt, rs, hb = xts[k % 3], rss[k % 2], hbs[k % 2]
                for c in range(16):
                    tmp = tmps[cnt[0] % 2]
                    cnt[0] += 1
                    self.stt(tmp.t[:, 0:n], xt.t[:, c, 0:n], g_mA.t[:, i, which, c, rr:rr + 1], rs.t[:, 0:n],
                             ALU.mult, ALU.mult, r=[xt.res, rs.res, g_mA.res], w=[tmp.res])
                    self.actf(hb.t[:, c, 0:n], tmp.t[:, 0:n], AF.Identity, r=[tmp.res, g_mod.res], pw=[hb.res],
                              bias=g_mod.t[:, i, shb + c, rr:rr + 1])
                fw.dma(fw.sp, dst[:, :, t0:t0 + n], hb.t[:, :, 0:n], hb.res, r=[hb.res], pw=dres)

            ld(0)
            if nt_ > 1:
                ld(1)
            s1(0)
            s2(0)
            for k in range(nt_):
                if k + 2 < nt_:
                    ld(k + 2)
                if k + 1 < nt_:
                    s1(k + 1)
                ap(k)
                if k + 1 < nt_:
                    s2(k + 1)
            fw.barrier()

    def load_in(self, st, src, sres, KC, Lx, name="inT"):
        fw = self.fw
        t = fw.sb(st, name, [128, KC, Lx], BF16, dma=True)
        step = 4
        t.kres = []
        for k0 in range(0, KC, step):
            k1 = min(KC, k0 + step)
            gr = fw.res(f"{name}_g{k0}", dma=True)
            fw.dma(fw.sp, t.t[:, k0:k1, :], src[:, k0:k1, :], gr, r=sres, w=[gr])
            t.kres += [gr] * (k1 - k0)
        return t

    def linear(self, st, inT, KC, Lx, W, parts, MC, SUP, epi, nw=3):
        fw = self.fw
        P = len(parts)
        nbk = (Lx + 511) // 512
        per = P * nbk
        nsets = max(1, 8 // per)
        wv = W.rearrange("(k p) n -> p k n", p=128)
        assert KC * P * SUP * 128 <= WR_ELEMS
        for sup in range(MC // SUP):
            wr = self.wring[self.wr_idx % NWR]
            self.wr_idx += 1
            wb = T(wr.t[:, 0:KC * P * SUP * 128].rearrange("p (k q s) -> p k q s", k=KC, q=P), wr.res)
            for p in range(P):
                c0 = parts[p] + sup * SUP * 128
                fw.dma(fw.pool, wb.t[:, :, p, :], wv[:, :, c0:c0 + SUP * 128], wb.res, w=[wb.res] if p == 0 else (), pw=() if p == 0 else [wb.res])
            for mm_ in range(SUP):
                m = sup * SUP + mm_
                base = (m % nsets) * per
                for p in range(P):
                    b0 = base + p * nbk
                    for nt in range(nbk):
                        n0 = nt * 512
                        n = min(512, Lx - n0)
                        for k in range(KC):
                            self.mm(self.psw(b0 + nt, n), wb.t[:, k, p, mm_ * 128:(mm_ + 1) * 128], inT.t[:, k, n0:n0 + n],
                                    start=(k == 0), stop=(k == KC - 1),
                                    r=[wb.res, inT.kres[k] if inT.kres else inT.res], w=[self.bank[b0 + nt]],
                                    sig=(k == KC - 1))
                    epi(m, p, self.psw(b0, Lx), self.banks(b0, Lx))

    def resid_epi(self, st, xview, xres, Lx, i, gbase, rr, bias_name):
        fw = self.fw
        g_mod = self.g_mod
        xin = [fw.sb(st, "xi", [128, Lx], F32, dma=True) for _ in range(3)]
        xo = [fw.sb(st, "xo", [128, Lx], F32, dma=True) for _ in range(2)]
        gb = None
        if bias_name is not None:
            gb = fw.sb(st, "gb", [128, 16], F32)
            self.tt(gb.t[:, :], g_mod.t[:, i, gbase:gbase + 16, rr], self.pvc(bias_name, 0, 16), ALU.mult,
                    r=[g_mod.res, self.g_pv.res], w=[gb.res])
        cnt = [0]

        def epi(m, p, ps, bks):
            xi = xin[cnt[0] % 3]
            o = xo[cnt[0] % 2]
            cnt[0] += 1
            fw.dma(fw.sp, xi.t[:], xview[:, m, :], xi.res, r=[xres[m]], w=[xi.res])
            self.stt(o.t[:], ps, g_mod.t[:, i, gbase + m, rr:rr + 1], xi.t[:], ALU.mult, ALU.add,
                     r=list(bks) + [xi.res, g_mod.res], w=[o.res])
            if gb is not None:
                self.actf(o.t[:], o.t[:], AF.Identity, r=[o.res, gb.res], w=[o.res], bias=gb.t[:, m:m + 1])
            fw.dma(fw.sp, xview[:, m, :], o.t[:], o.res, r=[o.res], pw=[xres[m]])
        return epi

    def scr_view(self, name, Lx, t0=0):
        return getattr(self, name).rearrange("(c p) t -> p c t", p=128)[:, :, t0:t0 + Lx]

    def conformer(self, i, j, hview, hres, xview, xres, Lx, rr, nseq=1):
        fw = self.fw
        UTv, VTv = self.scr_view("UT", Lx), self.scr_view("VT", Lx)
        rU, rV = self.r_scr["UT"], self.r_scr["VT"]
        with ExitStack() as st:
            inT = self.load_in(st, hview, [hres], 16, Lx)
            ta = [fw.sb(st, "ta", [128, Lx], F32) for _ in range(2)]
            sg = [fw.sb(st, "sg", [128, Lx], F32) for _ in range(2)]
            uo = [fw.sb(st, "uo", [128, Lx], F32, dma=True) for _ in range(2)]

            def epi(m, p, ps, bks):
                if p == 0:
                    self.actf(ta[m % 2].t[:], ps, AF.Identity, r=list(bks) + [self.g_pv.res], w=[ta[m % 2].res],
                              bias=self.pvc(f"abpw1{j}", m))
                else:
                    self.actf(sg[m % 2].t[:], ps, AF.Sigmoid, r=list(bks) + [self.g_pv.res], w=[sg[m % 2].res],
                              bias=self.pvc(f"abpw1{j}", 16 + m))
                    o = uo[m % 2]
                    self.tt(o.t[:], ta[m % 2].t[:], sg[m % 2].t[:], ALU.mult, r=[ta[m % 2].res, sg[m % 2].res], w=[o.res])
                    fw.dma(fw.sp, UTv[:, m, :], o.t[:], o.res, r=[o.res], pw=[rU])
            self.linear(st, inT, 16, Lx, self.a_w_pw1[j], [0, D], 16, 2, epi)
            fw.barrier()
        with ExitStack() as so:
            mean = fw.sb(so, "mean", [128, Lx], F32)
            rstd = fw.sb(so, "rstd", [128, Lx], F32)
            nbk = (Lx + 511) // 512
            with ExitStack() as st:
                ubf = [fw.sb(st, "ubf", [128, Lx], F32, dma=True) for _ in range(2)]
                Ls = Lx // nseq
                ubp = [fw.sb(st, "ubp", [128, nseq, Ls + 30], BF16) for _ in range(2)]
                dg = [fw.sb(st, "dg", [128, 31, 128], BF16) for _ in range(2)]
                idb = fw.sb(st, "idb", [128, 128], BF16)
                acc = [fw.sb(st, "acc", [128, Lx], F32, dma=True) for _ in range(2)]
                vb = [fw.sb(st, "vb", [128, 512], BF16) for _ in range(3)]
                vq = [fw.sb(st, "vq", [128, 512], BF16) for _ in range(3)]
                sacc = fw.sb(st, "sacc", [128, Lx], F32)
                qacc = fw.sb(st, "qacc", [128, Lx], F32)
                self.cp(fw.dve, idb.t[:], self.g_id.t[:], r=[self.g_id.res], w=[idb.res])
                for u in ubp:
                    fw.op(fw.dve, lambda e, u=u: e.memset(u.t[:, :, 0:15], 0.0), pw=[u.res])
                    fw.op(fw.dve, lambda e, u=u: e.memset(u.t[:, :, Ls + 15:Ls + 30], 0.0), pw=[u.res])
                if nseq == 1:
                    ctiles = [(n0, min(512, Lx - n0), [(0, n0, min(512, Lx - n0), 0)]) for n0 in range(0, Lx, 512)]
                else:
                    assert nseq * Ls <= 512
                    ctiles = [(0, Lx, [(s_, 0, Ls, s_ * Ls) for s_ in range(nseq)])]
                state = {"ti": 0, "si": 0, "pend": None}

                def emit_stats(item):
                    c, n0, n, vb_, vq_ = item
                    s_ = state["si"] % 2
                    state["si"] += 1
                    self.mm(self.psw(4 + s_, n), self.g_ones.t[:], vb_.t[:, 0:n], start=True, stop=True,
                            r=[vb_.res, self.g_ones.res], w=[self.bank[4 + s_]], sig=True)
                    self.mm(self.psw(6 + s_, n), self.g_ones.t[:], vq_.t[:, 0:n], start=True, stop=True,
                            r=[vq_.res, self.g_ones.res], w=[self.bank[6 + s_]], sig=True)
                    if c == 0:
                        self.cp(fw.dve, sacc.t[:, n0:n0 + n], self.psw(4 + s_, n), r=[self.bank[4 + s_]], pw=[sacc.res])
                        self.cp(fw.dve, qacc.t[:, n0:n0 + n], self.psw(6 + s_, n), r=[self.bank[6 + s_]], pw=[qacc.res])
                    else:
                        self.tt(sacc.t[:, n0:n0 + n], self.psw(4 + s_, n), sacc.t[:, n0:n0 + n], ALU.add,
                                r=[self.bank[4 + s_], sacc.res], w=[sacc.res])
                        self.tt(qacc.t[:, n0:n0 + n], self.psw(6 + s_, n), qacc.t[:, n0:n0 + n], ALU.add,
                                r=[self.bank[6 + s_], qacc.res], w=[qacc.res])

                def c2_load(c):
                    fw.dma(fw.sp, ubf[c % 2].t[:], UTv[:, c, :], ubf[c % 2].res, r=[rU], w=[ubf[c % 2].res])

                def c2_prep(c):
                    self.cp(fw.act, ubp[c % 2].t[:, :, 15:15 + Ls], ubf[c % 2].t[:].rearrange("p (s l) -> p s l", s=nseq),
                            r=[ubf[c % 2].res], w=[ubp[c % 2].res])
                    for k in range(31):
                        self.ts(dg[c % 2].t[:, k, :], idb.t[:], self.pvc(f"awdw{j}", k * 16 + c), None, ALU.mult, None,
                                r=[idb.res, self.g_pv.res], pw=[dg[c % 2].res])

                c2_load(0)
                c2_prep(0)
                for c in range(16):
                    uf, up, d_, a = ubf[c % 2], ubp[c % 2], dg[c % 2], acc[c % 2]
                    if c + 1 < 16:
                        c2_load(c + 1)
                    bdw = self.pvc(f"abdw{j}", c)
                    for nt, (n0, n, pieces) in enumerate(ctiles):
                        bk = state["ti"] % 4
                        vb_, vq_ = vb[state["ti"] % 3], vq[state["ti"] % 3]
                        state["ti"] += 1
                        for (s_, o_, ln_, pc_) in pieces:
                            for k in range(31):
                                self.mm(self.ps[:, bk * 512 + pc_:bk * 512 + pc_ + ln_], d_.t[:, k, :],
                                        up.t[:, s_, k + o_:k + o_ + ln_], start=(k == 0), stop=(k == 30),
                                        r=[d_.res, up.res], w=[self.bank[bk]], sig=(k == 30))
                        rr_ = [self.bank[bk], self.g_pv.res]
                        self.actf(a.t[:, n0:n0 + n], self.psw(bk, n), AF.Identity, r=rr_, pw=[a.res], bias=bdw)
                        self.actf(vb_.t[:, 0:n], self.psw(bk, n), AF.Identity, r=rr_, w=[vb_.res], bias=bdw)
                        self.actf(vq_.t[:, 0:n], self.psw(bk, n), AF.Square, r=rr_, w=[vq_.res], bias=bdw)
                        if state["pend"] is not None:
                            emit_stats(state["pend"])
                        state["pend"] = (c, n0, n, vb_, vq_)
                        if c + 1 < 16 and nt == min(1, len(ctiles) - 1):
                            c2_prep(c + 1)
                    fw.dma(fw.sp, VTv[:, c, :], a.t[:], a.res, r=[a.res], pw=[rV])
                emit_stats(state["pend"])
                msq = fw.sb(st, "msq", [128, Lx], F32)
                var = fw.sb(st, "var", [128, Lx], F32)
                self.actf(mean.t[:], sacc.t[:], AF.Identity, r=[sacc.res], w=[mean.res], scale=1.0 / D)
                self.tt(msq.t[:], mean.t[:], mean.t[:], ALU.mult, r=[mean.res], w=[msq.res])
                self.stt(var.t[:], qacc.t[:], 1.0 / D, msq.t[:], ALU.mult, ALU.subtract,
                         r=[qacc.res, msq.res], w=[var.res])
                self.actf(msq.t[:], var.t[:], AF.Sqrt, r=[var.res], w=[msq.res], bias=EPS)
                self.recip(rstd.t[:], msq.t[:], r=[msq.res], w=[rstd.res])
                fw.barrier()
            with ExitStack() as st:
                zT = fw.sb(st, "zT", [128, 16, Lx], BF16)
                zT.kres = [fw.res(f"z{c}") for c in range(16)]
                vin = [fw.sb(st, "vin", [128, Lx], F32, dma=True) for _ in range(2)]
                t1 = [fw.sb(st, "t1", [128, Lx], F32) for _ in range(2)]
                for c in range(16):
                    v = vin[c % 2]
                    t = t1[c % 2]
                    fw.dma(fw.sp, v.t[:], VTv[:, c, :], v.res, r=[rV], w=[v.res])
                    self.tt(t.t[:], v.t[:], mean.t[:], ALU.subtract, r=[v.res, mean.res], w=[t.res])
                    self.tt(t.t[:], t.t[:], rstd.t[:], ALU.mult, r=[t.res, rstd.res], w=[t.res])
                    self.actf(zT.t[:, c, :], t.t[:], AF.Silu, r=[t.res, self.g_pv.res], w=[zT.kres[c]],
                              bias=self.pvc(f"alnb{j}", c), scale=self.pvc(f"alng{j}", c))
                epi = self.resid_epi(st, xview, xres, Lx, i, 32, rr, f"abpw2{j}")
                self.linear(st, zT, 16, Lx, self.a_w_pw2[j], [0], 16, 2, epi)
                fw.barrier()

    def ffn(self, i, hview, hres, xview, xres, Lx, rr, nseq=1):
        fw = self.fw
        GTv = self.scr_view("GT", Lx)
        rG = self.r_scr["GT"]
        with ExitStack() as st:
            inT = self.load_in(st, hview, [hres], 16, Lx)
            yv = [fw.sb(st, "yv", [128, Lx], F32) for _ in range(2)]
            yg = [fw.sb(st, "yg", [128, Lx], F32) for _ in range(2)]
            go = [fw.sb(st, "go", [128, Lx], BF16, dma=True) for _ in range(2)]

            def epi(m, p, ps, bks):
                ch = m + 44 * p
                y = (yv if p == 0 else yg)[m % 2]
                rs_ = list(bks) + [self.g_pv.res]
                self.actf(y.t[:], ps, AF.Identity, r=rs_, w=[y.res], bias=self.pvc(f"fbdw{i}", ch),
                          scale=self.pvc(f"fwdw{i}", 88 + ch))
                Ls = Lx // nseq
                for s_ in range(nseq):
                    a_, b_ = s_ * Ls, (s_ + 1) * Ls
                    self.stt(y.t[:, a_ + 1:b_], ps[:, a_:b_ - 1], self.pvc(f"fwdw{i}", ch), y.t[:, a_ + 1:b_], ALU.mult, ALU.add,
                             r=rs_ + [y.res], w=[y.res])
                    self.stt(y.t[:, a_:b_ - 1], ps[:, a_ + 1:b_], self.pvc(f"fwdw{i}", 176 + ch), y.t[:, a_:b_ - 1], ALU.mult, ALU.add,
                             r=rs_ + [y.res], w=[y.res])
                if p == 1:
                    self.actf(y.t[:], y.t[:], AF.Silu, r=[y.res], w=[y.res])
                    o = go[m % 2]
                    self.tt(o.t[:], y.t[:], yv[m % 2].t[:], ALU.mult, r=[y.res, yv[m % 2].res], w=[o.res])
                    fw.dma(fw.sp, GTv[:, m, :], o.t[:], o.res, r=[o.res], pw=[rG])
            self.linear(st, inT, 16, Lx, self.f_w_up[i], [0, DFF], 44, 2, epi)
            fw.barrier()
        Lh = min(1024, Lx)
        for h0 in range(0, Lx, Lh):
            with ExitStack() as st:
                inT = self.load_in(st, self.scr_view("GT", Lh, h0), [rG], 44, Lh)
                epi = self.resid_epi(st, xview[:, :, h0:h0 + Lh], xres, Lh, i, 80, rr, None)
                self.linear(st, inT, 44, Lh, self.f_w_down[i], [0], 16, 1, epi)
                fw.barrier()

    def ctm_view(self):
        return self.CT.rearrange("(c p) t -> p c t", p=128)

    def htc_view(self):
        return self.HTC.rearrange("(c p) t -> p c t", p=128)

    def layer(self, i):
        kind = i % 3
        j = i // 3
        ctx_out = any(l % 3 == 2 for l in range(i + 1, DEPTH))
        ctx_in = ctx_out or kind == 2
        upto = self.upto
        CM = NB * CL
        lat = [(self.xt_view(b), self.r_XT[b], L, b, self.ht_view(b, False), self.r_HT[b]) for b in range(NB)]
        ctm = (self.ctm_view(), self.r_CT[0], CM, 2, self.htc_view(), self.r_HTC)
        jobs = [(xv, xr, hv, [hr], Lx, rr) for (xv, xr, Lx, rr, hv, hr) in lat]
        if ctx_in:
            if kind == 2:
                jobs += [(self.ct_view(b), self.r_CT[b], self.ht_view(b, True), [self.r_HT[b]], CL, 2) for b in range(NB)]
            else:
                jobs += [(ctm[0], ctm[1], ctm[4], [ctm[5]], CM, 2)]
        self.phase_norm_mod(jobs, i, 0)
        if upto == (i, "norm1"):
            return
        if kind == 0:
            for (xv, xr, Lx, rr, hv, hr) in lat:
                self.conformer(i, j, hv, hr, xv, xr, Lx, rr)
            if ctx_out:
                self.conformer(i, j, ctm[4], ctm[5], ctm[0], ctm[1], CM, 2, nseq=NB)
        elif kind == 1:
            for (xv, xr, Lx, rr, hv, hr) in lat:
                self.fnet(i, j, hv, hr, xv, xr, Lx, rr, 0)
            if ctx_out:
                self.fnet(i, j, ctm[4], ctm[5], ctm[0], ctm[1], CM, 2, 1, nseq=NB)
        else:
            for b in range(NB):
                self.attention(i, j, b)
        if upto == (i, "mixer"):
            return
        jobs = [(xv, xr, hv, [hr], Lx, rr) for (xv, xr, Lx, rr, hv, hr) in lat]
        if ctx_out:
            jobs += [(ctm[0], ctm[1], ctm[4], [ctm[5]], CM, 2)]
        self.phase_norm_mod(jobs, i, 1)
        for (xv, xr, Lx, rr, hv, hr) in lat:
            self.ffn(i, hv, hr, xv, xr, Lx, rr)
        if ctx_out:
            self.ffn(i, ctm[4], ctm[5], ctm[0], ctm[1], CM, 2, nseq=NB)

    def fnet(self, i, j, hview, hres, xview, xres, Lx, rr, v, nseq=1):
        fw = self.fw
        rAB, rY = self.r_scr["AB"], self.r_scr["YT"]
        LC = Lx // 128
        with ExitStack() as st:
            inT = self.load_in(st, hview, [hres], 16, Lx)
            cs = fw.sb(st, "cs", [128, 2, 512], BF16, dma=True)
            fw.dma(fw.sp, cs.t[:], self.dft_cs[v], cs.res, w=[cs.res])
            abo = [fw.sb(st, "abo", [128, 8, 512], BF16, dma=True) for _ in range(2)]
            nb = 0
            for tc in range(LC):
                o = abo[tc % 2]
                for g in range(8):
                    bk = nb % 8
                    nb += 1
                    for kk in range(2):
                        self.mm(self.psw(bk, 512), inT.t[:, 2 * g + kk, tc * 128:(tc + 1) * 128], cs.t[:, kk, :],
                                start=(kk == 0), stop=(kk == 1), r=[inT.kres[2 * g + kk], cs.res], w=[self.bank[bk]], sig=(kk == 1))
                    self.cp(fw.dve if g % 2 == 0 else fw.act, o.t[:, g, :], self.psw(bk, 512), r=[self.bank[bk]], pw=[o.res])
                fw.dma(fw.sp, self.AB[tc * 128:(tc + 1) * 128, :].rearrange("p (g n) -> p g n", g=8), o.t[:], o.res,
                       r=[o.res], pw=[rAB])
            fw.barrier()
        if nseq > 1:
            Ls = Lx // nseq
            LCs = Ls // 128
            with ExitStack() as st:
                cl = fw.sb(st, "cl", [128, LCs, Ls], BF16, dma=True)
                sl = fw.sb(st, "sl", [128, LCs, Ls], BF16, dma=True)
                srcm = self.dft_c
                fw.dma(fw.sp, cl.t[:], srcm[0], cl.res, w=[cl.res])
                fw.dma(fw.sp, sl.t[:], srcm[1], sl.res, w=[sl.res])
                abg = [fw.sb(st, "abg", [128, LC, 512], BF16, dma=True) for _ in range(2)]
                yo = [fw.sb(st, "yo", [128, Lx], BF16, dma=True) for _ in range(2)]
                ABv = self.AB.rearrange("(lc p) n -> p lc n", p=128)
                YTv = self.scr_view("YT", Lx)

                def ld_ab2(g):
                    fw.dma(fw.sp, abg[g % 2].t[:], ABv[:, 0:LC, g * 512:(g + 1) * 512], abg[g % 2].res, r=[rAB], w=[abg[g % 2].res])

                ld_ab2(0)
                for g in range(8):
                    a = abg[g % 2]
                    if g + 1 < 8:
                        ld_ab2(g + 1)
                    for hf in range(2):
                        m = 2 * g + hf
                        bk = m % 8
                        for s_ in range(nseq):
                            pcols = self.ps[:, bk * 512 + s_ * Ls:bk * 512 + (s_ + 1) * Ls]
                            for lc in range(LCs):
                                self.mm(pcols, a.t[:, s_ * LCs + lc, hf * 128:(hf + 1) * 128], cl.t[:, lc, :],
                                        start=(lc == 0), stop=False, r=[a.res, cl.res], w=[self.bank[bk]])
                                self.mm(pcols, a.t[:, s_ * LCs + lc, 256 + hf * 128:256 + (hf + 1) * 128], sl.t[:, lc, :],
                                        start=False, stop=(lc == LCs - 1), r=[a.res, sl.res], w=[self.bank[bk]],
                                        sig=(lc == LCs - 1))
                        o = yo[m % 2]
                        self.cp(fw.dve if m % 2 == 0 else fw.act, o.t[:], self.psw(bk, Lx), r=[self.bank[bk]], w=[o.res])
                        fw.dma(fw.sp, YTv[:, m, :], o.t[:], o.res, r=[o.res], pw=[rY])
                fw.barrier()
        if nseq == 1:
            with ExitStack() as st:
                nh = 2 if Lx >= 1024 else 1
                Lh = Lx // nh
                cl = fw.sb(st, "cl", [128, LC, Lh], BF16, dma=True)
                sl = fw.sb(st, "sl", [128, LC, Lh], BF16, dma=True)
                srcm = self.dft_l if v == 0 else self.dft_c
                abg = [fw.sb(st, "abg", [128, LC, 512], BF16, dma=True) for _ in range(2)]
                yo = [fw.sb(st, "yo", [128, Lh], BF16, dma=True) for _ in range(2)]
                ABv = self.AB.rearrange("(lc p) n -> p lc n", p=128)
                YTv = self.scr_view("YT", Lx)
                nbk = (Lh + 511) // 512
                seq = [(h, g) for h in range(nh) for g in range(8)]

                def ld_ab(k):
                    g = seq[k][1]
                    fw.dma(fw.sp, abg[k % 2].t[:], ABv[:, 0:LC, g * 512:(g + 1) * 512], abg[k % 2].res, r=[rAB], w=[abg[k % 2].res])

                def ld_dft(h):
                    for k0 in range(0, LC, 4):
                        k1 = min(LC, k0 + 4)
                        fw.dma(fw.sp, cl.t[:, k0:k1, :], srcm[0][:, k0:k1, h * Lh:(h + 1) * Lh], cl.res, pw=[cl.res])
                        fw.dma(fw.sp, sl.t[:, k0:k1, :], srcm[1][:, k0:k1, h * Lh:(h + 1) * Lh], sl.res, pw=[sl.res])

                ld_ab(0)
                cnt = 0
                for k, (h, g) in enumerate(seq):
                    if g == 0:
                        ld_dft(h)
                    a = abg[k % 2]
                    if k + 1 < len(seq):
                        ld_ab(k + 1)
                    for hf in range(2):
                        m = 2 * g + hf
                        b0 = (cnt % (8 // nbk)) * nbk
                        o = yo[cnt % 2]
                        cnt += 1
                        for nt in range(nbk):
                            n0 = nt * 512
                            n = min(512, Lh - n0)
                            for lc in range(LC):
                                self.mm(self.psw(b0 + nt, n), a.t[:, lc, hf * 128:(hf + 1) * 128], cl.t[:, lc, n0:n0 + n],
                                        start=(lc == 0), stop=False, r=[a.res, cl.res], w=[self.bank[b0 + nt]])
                                self.mm(self.psw(b0 + nt, n), a.t[:, lc, 256 + hf * 128:256 + (hf + 1) * 128], sl.t[:, lc, n0:n0 + n],
                                        start=False, stop=(lc == LC - 1), r=[a.res, sl.res], w=[self.bank[b0 + nt]],
                                        sig=(lc == LC - 1))
                        self.cp(fw.dve if m % 2 == 0 else fw.act, o.t[:], self.psw(b0, Lh), r=self.banks(b0, Lh), w=[o.res])
                        fw.dma(fw.sp, YTv[:, m, h * Lh:(h + 1) * Lh], o.t[:], o.res, r=[o.res], pw=[rY])
                fw.barrier()
        with ExitStack() as st:
            inT = self.load_in(st, self.scr_view("YT", Lx), [rY], 16, Lx)
            epi = self.resid_epi(st, xview, xres, Lx, i, 32, rr, "bbout")
            self.linear(st, inT, 16, Lx, self.b_w_out[j], [0], 16, 2, epi)
            fw.barrier()

    def attention(self, i, j, b):
        fw = self.fw
        LT = L + CL
        hv = self.HT.rearrange("(c p) t -> p c t", p=128)[:, :, b * LT:(b + 1) * LT]
        hres = [self.r_HT[b]]
        rQ, rK, rVS, rO = (self.r_scr[n] for n in ("QT", "KT", "VS", "OT"))
        QTv = self.scr_view("QT", L)
        KTv = self.scr_view("KT", LT)
        wq = self.c_w_qkv[j].rearrange("(k p) n -> p k n", p=128)
        with ExitStack() as st:
            inT = self.load_in(st, hv, hres, 16, LT)
            wcur = {}
            epsq = fw.sb(st, "epsq", [128, 2], F32)
            fw.op(fw.dve, lambda e: e.memset(epsq.t[:, 0:1], HD * EPS), pw=[epsq.res])
            fw.op(fw.dve, lambda e: e.memset(epsq.t[:, 1:2], EPS), pw=[epsq.res])
            sq = [fw.sb(st, "sq", [128, 512], BF16) for _ in range(4)]
            sd = [fw.sb(st, "sd", [128, 512], F32) for _ in range(4)]
            rs = [fw.sb(st, "rs", [128, 512], F32) for _ in range(4)]
            qo = [fw.sb(st, "qo", [128, 512], BF16, dma=True) for _ in range(4)]
            items = []
            for m in range(32):
                tiles = [(t_, 512) for t_ in range(0, L, 512)] + ([(2048, 256)] if m >= 16 else [])
                for (t0, n) in tiles:
                    items.append((m, t0, n))

            def emit_main(idx):
                m, t0, n = items[idx]
                s = idx % 4
                if m % 2 == 0 and t0 == 0:
                    wr = self.wring[self.wr_idx % NWR]
                    self.wr_idx += 1
                    wcur[m // 2] = T(wr.t[:, 0:16 * 256].rearrange("p (k s) -> p k s", k=16), wr.res)
                    fw.dma(fw.pool, wcur[m // 2].t[:], wq[:, :, m * 128:(m + 2) * 128], wr.res, w=[wr.res])
                wb = wcur[m // 2]
                for k in range(16):
                    self.mm(self.psw(s * 2, n), wb.t[:, k, (m % 2) * 128:(m % 2 + 1) * 128],
                            inT.t[:, k, t0:t0 + n], start=(k == 0), stop=(k == 15),
                            r=[wb.res, inT.kres[k]], w=[self.bank[s * 2]], sig=(k == 15))
                self.actf(sq[s].t[:, 0:n], self.psw(s * 2, n), AF.Square, r=[self.bank[s * 2]], w=[sq[s].res])

            def emit_stat(idx):
                m, t0, n = items[idx]
                s = idx % 4
                isq = m < 16
                self.mm(self.psw(s * 2 + 1, n), self.g_ones.t[:], sq[s].t[:, 0:n], start=True, stop=True,
                        r=[sq[s].res, self.g_ones.res], w=[self.bank[s * 2 + 1]], sig=True)
                if isq:
                    self.actf(sd[s].t[:, 0:n], self.psw(s * 2 + 1, n), AF.Ln, r=[self.bank[s * 2 + 1], epsq.res], w=[sd[s].res],
                              bias=epsq.t[:, 0:1], scale=1.0)
                else:
                    self.actf(sd[s].t[:, 0:n], self.psw(s * 2 + 1, n), AF.Ln, r=[self.bank[s * 2 + 1], epsq.res], w=[sd[s].res],
                              bias=epsq.t[:, 1:2], scale=1.0 / HD)
                self.actf(rs[s].t[:, 0:n], sd[s].t[:, 0:n], AF.Exp, r=[sd[s].res], w=[rs[s].res], scale=-0.5)
                self.stt(qo[s].t[:, 0:n], self.psw(s * 2, n), self.pvc("cqg" if isq else "ckg"), rs[s].t[:, 0:n],
                         ALU.mult, ALU.mult, r=[self.bank[s * 2], rs[s].res, self.g_pv.res], w=[qo[s].res])
                if isq:
                    fw.dma(fw.sp, QTv[:, m, t0:t0 + n], qo[s].t[:, 0:n], qo[s].res, r=[qo[s].res], pw=[rQ])
                else:
                    fw.dma(fw.sp, KTv[:, m - 16, t0:t0 + n], qo[s].t[:, 0:n], qo[s].res, r=[qo[s].res], pw=[rK])

            for idx in range(len(items)):
                emit_main(idx)
                if idx > 0:
                    emit_stat(idx - 1)
            emit_stat(len(items) - 1)
            vo = [fw.sb(st, "vo", [128, 512], BF16, dma=True) for _ in range(4)]
            nb = 0
            for nt in range(4):
                wr = self.wring[self.wr_idx % NWR]
                self.wr_idx += 1
                wv = T(wr.t[:, 0:16 * 512].rearrange("p (k s) -> p k s", k=16), wr.res)
                fw.dma(fw.pool, wv.t[:], wq[:, :, 2 * D + nt * 512:2 * D + (nt + 1) * 512], wr.res, w=[wr.res])
                for tc in range(LT // 128):
                    bk = nb % 8
                    o = vo[nb % 4]
                    nb += 1
                    for k in range(16):
                        self.mm(self.psw(bk, 512), inT.t[:, k, tc * 128:(tc + 1) * 128], wv.t[:, k, :],
                                start=(k == 0), stop=(k == 15), r=[inT.kres[k], wv.res], w=[self.bank[bk]], sig=(k == 15))
                    self.cp(fw.dve if nb % 2 == 0 else fw.act, o.t[:], self.psw(bk, 512), r=[self.bank[bk]], w=[o.res])
                    fw.dma(fw.sp, self.VS[tc * 128:(tc + 1) * 128, nt * 512:(nt + 1) * 512], o.t[:], o.res, r=[o.res], pw=[rVS])
            fw.barrier()
        with ExitStack() as st:
            qh = [fw.sb(st, "qh", [128, L], BF16, dma=True) for _ in range(2)]
            kh = [fw.sb(st, "kh", [128, LT], BF16, dma=True) for _ in range(2)]
            vh = [fw.sb(st, "vh", [128, LT // 128, 128], BF16, dma=True) for _ in range(2)]
            bt = [fw.sb(st, "bt", [128, N_BT * 512], F32, dma=True) for _ in range(2)]
            sbf = [fw.sb(st, "sbf", [128, 512], F32) for _ in range(3)]
            pb = [fw.sb(st, "pb", [128, 512], BF16) for _ in range(9)]
            rc = [fw.sb(st, "rc", [128, 512], F32) for _ in range(2)]
            lnb = [fw.sb(st, "lnb", [128, 512], F32) for _ in range(2)]
            oo = [fw.sb(st, "oo", [128, 512], BF16, dma=True) for _ in range(2)]
            VSv = self.VS.rearrange("(tc p) d -> p tc d", p=128)
            OTv = self.scr_view("OT", L)
            cnt_s = 0
            cnt_p = 0
            grp = 0
            def load_qkb(h):
                q_, k_, b_ = qh[h % 2], kh[h % 2], bt[h % 2]
                fw.dma(fw.sp, q_.t[:], QTv[:, h, :], q_.res, r=[rQ], w=[q_.res])
                fw.dma(fw.sp, k_.t[:], KTv[:, h, :], k_.res, r=[rK], w=[k_.res])
                fw.dma(fw.sp, b_.t[:], self.att_bias[h], b_.res, w=[b_.res])

            def load_v(h):
                v_ = vh[h % 2]
                fw.dma(fw.sp, v_.t[:], VSv[:, :, h * 128:(h + 1) * 128], v_.res, r=[rVS], w=[v_.res])

            def load_head(h):
                load_qkb(h)
                load_v(h)

            items = []
            for h in range(NH):
                for g in range(4):
                    chunks = [(kc, ATT_TILE0[g] + ci) for ci, kc in enumerate(ATT_CHUNKS[g])] + [(16, None), (17, None)]
                    for ci, (kc, tile) in enumerate(chunks):
                        items.append((h, g, kc, tile, ci == 0, ci == len(chunks) - 1))
            LAG = 6
            NPB = len(pb)
            load_head(0)
            load_head(1)
            for idx in range(len(items) + LAG):
                if idx < len(items):
                    h, g, kc, tile, first, last = items[idx]
                    q_, k_, b_ = qh[h % 2], kh[h % 2], bt[h % 2]
                    sbk = idx % 4
                    self.mm(self.psw(sbk, 512), k_.t[:, kc * 128:(kc + 1) * 128], q_.t[:, g * 512:(g + 1) * 512],
                            start=True, stop=True, r=[k_.res, q_.res], w=[self.bank[sbk]], sig=True)
                    p_ = pb[idx % NPB]
                    if tile is not None:
                        sf = sbf[idx % 3]
                        self.tt(sf.t[:], self.psw(sbk, 512), b_.t[:, tile * 512:(tile + 1) * 512], ALU.add,
                                r=[self.bank[sbk], b_.res], w=[sf.res])
                        self.actf(p_.t[:], sf.t[:], AF.Exp, r=[sf.res], w=[p_.res])
                    else:
                        self.actf(p_.t[:], self.psw(sbk, 512), AF.Exp, r=[self.bank[sbk]], w=[p_.res])
                    if last and g == 3 and h + 2 < NH:
                        load_qkb(h + 2)
                if idx >= LAG:
                    jn = idx - LAG
                    h, g, kc, tile, first, last = items[jn]
                    v_ = vh[h % 2]
                    p_ = pb[jn % NPB]
                    grp = h * 4 + g
                    ob = 4 + (grp % 2)
                    db = 6 + (grp % 2)
                    self.mm(self.psw(ob, 512), v_.t[:, kc, :], p_.t[:], start=first, stop=last,
                            r=[v_.res, p_.res], w=[self.bank[ob]], sig=last)
                    self.mm(self.psw(db, 512), self.g_ones.t[:], p_.t[:], start=first, stop=last,
                            r=[self.g_ones.res, p_.res], w=[self.bank[db]], sig=last)
                    if last:
                        r_, o_, l_ = rc[grp % 2], oo[grp % 2], lnb[grp % 2]
                        self.actf(l_.t[:], self.psw(db, 512), AF.Ln, r=[self.bank[db]], w=[l_.res])
                        self.actf(r_.t[:], l_.t[:], AF.Exp, r=[l_.res], w=[r_.res], scale=-1.0)
                        self.tt(o_.t[:], self.psw(ob, 512), r_.t[:], ALU.mult, r=[self.bank[ob], r_.res], w=[o_.res])
                        fw.dma(fw.sp, OTv[:, h, g * 512:(g + 1) * 512], o_.t[:], o_.res, r=[o_.res], pw=[rO])
                        if g == 3 and h + 2 < NH:
                            load_v(h + 2)
            fw.barrier()
        with ExitStack() as st:
            inT = self.load_in(st, self.scr_view("OT", L), [rO], 16, L)
            epi = self.resid_epi(st, self.xt_view(b), self.r_XT[b], L, i, 32, b, None)
            self.linear(st, inT, 16, L, self.c_w_o[j], [0], 16, 2, epi)
            fw.barrier()


def fm(v):
    return np.ascontiguousarray(np.asarray(v, np.float32).reshape(-1, 128).T)


def build_pvec(inp):
    pv = np.zeros((128, PVL.n), np.float32)

    def put(name, a):
        o = PVL.off[name]
        pv[:, o:o + a.shape[1]] = a

    for i in range(DEPTH):
        put(f"n1g{i}", fm(inp["norm1_g"][i]))
        put(f"n2g{i}", fm(inp["norm2_g"][i]))
        put(f"modb{i}", fm(inp["mod_b"][i]))
        put(f"fwdw{i}", np.concatenate([fm(inp["f_w_dw"][i][k]) for k in range(3)], axis=1))
        put(f"fbdw{i}", fm(inp["f_b_dw"][i]))
    for j in range(2):
        put(f"abpw1{j}", fm(inp["a_b_pw1"][j]))
        put(f"awdw{j}", np.concatenate([fm(inp["a_w_dw"][j][k]) for k in range(31)], axis=1))
        put(f"abdw{j}", fm(inp["a_b_dw"][j]))
        put(f"alng{j}", fm(inp["a_ln_g"][j]))
        put(f"alnb{j}", fm(inp["a_ln_b"][j]))
        put(f"abpw2{j}", fm(inp["a_b_pw2"][j]))
    put("bbout", fm(inp["b_b_out"][0]))
    put("cqg", np.asarray(inp["c_q_g"][0], np.float32)[:, None])
    put("ckg", np.asarray(inp["c_k_g"][0], np.float32)[:, None])
    return pv


def build_att_bias(rpb):
    rpb = np.asarray(rpb, np.float32)
    out = np.full((NH, 128, N_BT, 512), NEG, np.float32)
    rows = L // GRID_W
    rs = np.clip(np.arange(rows) - 4, 0, rows - 8)
    cs = np.clip(np.arange(GRID_W) - 8, 0, GRID_W - 16)
    p = np.arange(128)
    n = np.arange(512)
    for g in (0, 1, 3):
        for ci, kc in enumerate(ATT_CHUNKS[g]):
            tile = ATT_TILE0[g] + ci
            kr = (2 * kc + p // 64)[:, None]
            kcol = (p % 64)[:, None]
            qr = (8 * g + n // 64)[None, :]
            qcol = (n % 64)[None, :]
            valid = (kr >= rs[qr]) & (kr < rs[qr] + 8) & (kcol >= cs[qcol]) & (kcol < cs[qcol] + 16)
            dr = np.clip(kr - qr + 7, 0, 14)
            dc = np.clip(kcol - qcol, -15, 15) + 15
            vals = rpb[:, dr, dc]
            out[:, :, tile, :] = np.where(valid[None], vals, np.float32(NEG))
    return out.reshape(NH, 128, N_BT * 512)


def build_dft():
    bf = ml_dtypes.bfloat16
    p = np.arange(128)
    cidx = (np.arange(2)[None, :, None] * 128 + p[:, None, None]) * np.arange(256)[None, None, :]
    ang = 2.0 * np.pi * (cidx % 256) / 256.0
    cs = np.zeros((2, 128, 2, 512), np.float64)
    for v, ln in enumerate((L, CL)):
        nrm = 1.0 / np.sqrt(ln * 256.0)
        cs[v, :, :, 0:256] = np.cos(ang) * nrm
        cs[v, :, :, 256:512] = np.sin(ang) * nrm
    lidx = (np.arange(16)[None, :, None] * 128 + p[:, None, None]) * np.arange(L)[None, None, :]
    angl = 2.0 * np.pi * (lidx % L) / float(L)
    dl = np.stack([np.cos(angl), -np.sin(angl)])
    cidx2 = (np.arange(2)[None, :, None] * 128 + p[:, None, None]) * np.arange(CL)[None, None, :]
    angc = 2.0 * np.pi * (cidx2 % CL) / float(CL)
    dc = np.stack([np.cos(angc), -np.sin(angc)])
    return cs.astype(np.float32).astype(bf), dl.astype(np.float32).astype(bf), dc.astype(np.float32).astype(bf)


def shared_inputs(inp):
    cs, dl, dc = build_dft()
    sh = {
        "pvec": build_pvec(inp),
        "ident": np.eye(128, dtype=np.float32),
        "dft_cs": cs, "dft_l": dl, "dft_c": dc,
        "att_bias": build_att_bias(inp["c_rpb"][0]),
    }
    for k in ("mod_w", "a_w_pw1", "a_w_pw2", "b_w_out", "c_w_qkv", "c_w_o", "f_w_up", "f_w_down"):
        sh[k] = np.ascontiguousarray(np.asarray(inp[k], np.float32))
    return sh


def core_inputs(inp, core, sh):
    b0 = core * NB
    m = dict(sh)
    m["x"] = np.ascontiguousarray(np.asarray(inp["x"][b0:b0 + NB], np.float32).reshape(NB * L, D))
    m["ctx"] = np.ascontiguousarray(np.asarray(inp["ctx"][b0:b0 + NB], np.float32).reshape(NB * CL, D))
    rows = np.concatenate([np.asarray(inp["c"][b0:b0 + NB], np.float32), np.asarray(inp["c_ctx"], np.float32)[None, :]], axis=0)
    m["csil"] = np.ascontiguousarray(rows.reshape(3, 16, 128).transpose(2, 1, 0))
    return m


_PROG = {}


def kernel(**inputs):
    if "p" not in _PROG:
        _PROG["p"] = Prog()
    prog = _PROG["p"]
    sh = shared_inputs(inputs)
    in_maps = [core_inputs(inputs, c, sh) for c in range(NCORES)]
    res = run_bass_kernel_spmd(prog.nc, in_maps, core_ids=list(range(NCORES)))
    outs = [np.asarray(r["out"], np.float32).reshape(NB, L, D) for r in res.results]
    return np.concatenate(outs, axis=0)
```

```python
import bisect
from contextlib import ExitStack

import numpy as np
import ml_dtypes
import concourse.bass as bass
import concourse.mybir as mybir
from concourse.bass_utils import run_bass_kernel_spmd

F32 = mybir.dt.float32
BF16 = mybir.dt.bfloat16
AF = mybir.ActivationFunctionType
ALU = mybir.AluOpType

D = 2048
L = 2048
CL = 256
NB = 2
NCORES = 8
DFF = 5632
DEPTH = 4
EPS = 1e-6
GRID_W = 64
HD = 128
NH = 16
NEG = -30000.0


class Res:
    __slots__ = ("name", "w", "r", "sem")

    def __init__(self, name):
        self.name = name
        self.w = {}
        self.r = {}
        self.sem = None


class Sem:
    __slots__ = ("h", "count", "name")

    def __init__(self, h, name):
        self.h = h
        self.count = 0
        self.name = name


class Eng:
    def __init__(self, name, h, sem):
        self.name = name
        self.h = h
        self.sem = sem
        self.seq = 0
        self.sigseq = []
        self.waited = {}


class T:
    __slots__ = ("t", "res", "kres")

    def __init__(self, t, res):
        self.t = t
        self.res = res
        self.kres = None


class FW:
    N_DMA_SEMS = 92

    def __init__(self, nc, stack):
        self.nc = nc
        self.stack = stack
        mk = lambda n: Sem(stack.enter_context(nc.semaphore(n)), n)
        self.pe = Eng("pe", nc.tensor, mk("s_pe"))
        self.act = Eng("act", nc.scalar, mk("s_act"))
        self.dve = Eng("dve", nc.vector, mk("s_dve"))
        self.pool = Eng("pool", nc.gpsimd, mk("s_pool"))
        self.sp = Eng("sp", nc.sync, None)
        self.engs = [self.pe, self.act, self.dve, self.pool, self.sp]
        self.dsems = [mk(f"s_d{i}") for i in range(self.N_DMA_SEMS)]
        self.dsem_next = 0
        self.dsem_base = 0
        self.uid = 0
        self.all_events = {}
        self.n_waits = 0

    def res(self, name, dma=False):
        r = Res(name)
        if dma:
            assert self.dsem_next < self.N_DMA_SEMS, "out of DMA semaphores"
            r.sem = self.dsems[self.dsem_next]
            self.dsem_next += 1
        return r

    def sb(self, pstack, name, shape, dtype, dma=False):
        self.uid += 1
        t = pstack.enter_context(self.nc.sbuf_tensor(f"{name}_{self.uid}", shape, dtype))
        return T(t, self.res(name, dma))

    def _resolve(self, key, val):
        if isinstance(key, Eng):
            idx = bisect.bisect_left(key.sigseq, val)
            assert idx < len(key.sigseq), f"unsignaled dependency on {key.name} seq {val}"
            return key.sem, idx + 1
        return key, val

    def _wait(self, eng, deps):
        for key, val in deps.items():
            if key is eng and eng is self.pe:
                continue
            sem, v = self._resolve(key, val)
            if eng.waited.get(sem.name, 0) >= v:
                continue
            eng.h.wait_ge(sem.h, v)
            eng.waited[sem.name] = v
            self.n_waits += 1

    @staticmethod
    def _merge(dst, src):
        for k, v in src.items():
            if dst.get(k, 0) < v:
                dst[k] = v

    def _deps(self, r, w, pw):
        deps = {}
        for x in r:
            self._merge(deps, x.w)
        for x in w:
            self._merge(deps, x.w)
            self._merge(deps, x.r)
        for x in pw:
            self._merge(deps, x.r)
        return deps

    def _record(self, key, val, r, w, pw):
        for x in r:
            if x.r.get(key, 0) < val:
                x.r[key] = val
        for x in list(w) + list(pw):
            if x.w.get(key, 0) < val:
                x.w[key] = val
        if self.all_events.get(key, 0) < val:
            self.all_events[key] = val

    def op(self, eng, fn, r=(), w=(), pw=(), sig=True):
        self._wait(eng, self._deps(r, w, pw))
        ins = fn(eng.h)
        eng.seq += 1
        if sig:
            ins.then_inc(eng.sem.h, 1)
            eng.sigseq.append(eng.seq)
        self._record(eng, eng.seq, r, w, pw)
        return ins

    def dma(self, q, out, in_, semres, r=(), w=(), pw=()):
        self._wait(q, self._deps(r, w, pw))
        sem = semres.sem
        assert sem is not None, semres.name
        q.h.dma_start(out=out, in_=in_).then_inc(sem.h, 16)
        sem.count += 16
        self._record(sem, sem.count, r, w, pw)

    def barrier(self):
        for e in self.engs:
            if e is self.pool:
                continue
            self._wait(e, self.all_events)
        self.dsem_next = self.dsem_base

    def engine_sync(self, eng):
        self._wait(eng, self.all_events)

    def persist(self):
        self.dsem_base = self.dsem_next


class PV:
    def __init__(self):
        self.off = {}
        self.n = 0

    def add(self, name, ncols):
        self.off[name] = self.n
        self.n += ncols


def pv_layout():
    pv = PV()
    for i in range(DEPTH):
        pv.add(f"n1g{i}", 16)
        pv.add(f"n2g{i}", 16)
        pv.add(f"modb{i}", 96)
        pv.add(f"fwdw{i}", 3 * 88)
        pv.add(f"fbdw{i}", 88)
    for j in range(2):
        pv.add(f"abpw1{j}", 32)
        pv.add(f"awdw{j}", 31 * 16)
        pv.add(f"abdw{j}", 16)
        pv.add(f"alng{j}", 16)
        pv.add(f"alnb{j}", 16)
        pv.add(f"abpw2{j}", 16)
    pv.add("bbout", 16)
    pv.add("cqg", 1)
    pv.add("ckg", 1)
    return pv


PVL = pv_layout()

ATT_CHUNKS = {0: list(range(0, 6)), 1: list(range(2, 10)), 2: list(range(6, 14)), 3: list(range(10, 16))}
ATT_TILE0 = {0: 0, 1: 6, 2: 6, 3: 14}
N_BT = 20
WR_ELEMS = 8192
NWR = 2


class Prog:
    def __init__(self, nlayers=DEPTH, debug=(), upto=None):
        self.upto = upto
        self.nlayers = nlayers
        self.debug = set(debug)
        self.nc = bass.Bass("TRN2", target_bir_lowering=False)
        self.build()

    def din(self, name, shape, dtype=F32):
        return self.nc.dram_tensor(name, list(shape), dtype, kind="ExternalInput").ap()

    def dscr(self, name, shape, dtype):
        if name in self.debug:
            return self.nc.dram_tensor(name, list(shape), dtype, kind="ExternalOutput").ap()
        return self.nc.dram_tensor(name, list(shape), dtype).ap()

    def build(self):
        nc = self.nc
        self.x = self.din("x", [NB * L, D])
        self.ctx = self.din("ctx", [NB * CL, D])
        self.csil = self.din("csil", [128, 16, 3])
        self.pvec = self.din("pvec", [128, PVL.n])
        self.ident = self.din("ident", [128, 128])
        self.mod_w = self.din("mod_w", [DEPTH, D, 6 * D])
        self.a_w_pw1 = self.din("a_w_pw1", [2, D, 2 * D])
        self.a_w_pw2 = self.din("a_w_pw2", [2, D, D])
        self.b_w_out = self.din("b_w_out", [1, D, D])
        self.c_w_qkv = self.din("c_w_qkv", [1, D, 3 * D])
        self.c_w_o = self.din("c_w_o", [1, D, D])
        self.f_w_up = self.din("f_w_up", [DEPTH, D, 2 * DFF])
        self.f_w_down = self.din("f_w_down", [DEPTH, DFF, D])
        self.dft_cs = self.din("dft_cs", [2, 128, 2, 512], BF16)
        self.dft_l = self.din("dft_l", [2, 128, 16, L], BF16)
        self.dft_c = self.din("dft_c", [2, 128, 2, CL], BF16)
        self.att_bias = self.din("att_bias", [NH, 128, N_BT * 512])
        self.out = nc.dram_tensor("out", [NB * L, D], F32, kind="ExternalOutput").ap()
        self.XT = self.dscr("XT", [D, NB * L], F32)
        self.CT = self.dscr("CT", [D, NB * CL], F32)
        self.HT = self.dscr("HT", [D, NB * (L + CL)], BF16)
        self.HTC = self.dscr("HTC", [D, NB * CL], BF16)
        self.UT = self.dscr("UT", [D, L], F32)
        self.VT = self.dscr("VT", [D, L], F32)
        self.GT = self.dscr("GT", [DFF, L], BF16)
        self.AB = self.dscr("AB", [L, 8 * 512], BF16)
        self.YT = self.dscr("YT", [D, L], BF16)
        self.QT = self.dscr("QT", [D, L], BF16)
        self.KT = self.dscr("KT", [D, L + CL], BF16)
        self.VS = self.dscr("VS", [L + CL, D], BF16)
        self.OT = self.dscr("OT", [D, L], BF16)
        self.r_XT = [[Res(f"XT{b}_{c}") for c in range(16)] for b in range(NB)]
        _rct = [Res(f"CT_{c}") for c in range(16)]
        self.r_CT = [_rct for b in range(NB)]
        self.r_HTC = Res("HTC")
        self.r_HT = [Res(f"HT{b}") for b in range(NB)]
        self.r_scr = {n: Res(n) for n in ("UT", "VT", "GT", "AB", "YT", "QT", "KT", "VS", "OT")}

        with ExitStack() as gs:
            fw = self.fw = FW(nc, gs)
            self.ps = gs.enter_context(nc.psum_tensor("ps", [128, 4096], F32))
            self.bank = [fw.res(f"bank{i}") for i in range(8)]
            g_pv = fw.sb(gs, "pv", [128, PVL.n], F32, dma=True)
            g_id = fw.sb(gs, "ident", [128, 128], F32, dma=True)
            g_ones = fw.sb(gs, "ones", [128, 128], BF16)
            g_mod = fw.sb(gs, "mod", [128, DEPTH, 96, 3], F32)
            g_mA = fw.sb(gs, "modA", [128, DEPTH, 2, 16, 3], F32)
            self.g_pv, self.g_id, self.g_ones, self.g_mod, self.g_mA = g_pv, g_id, g_ones, g_mod, g_mA
            self.wring = [fw.sb(gs, "wring", [128, WR_ELEMS], BF16, dma=True) for _ in range(NWR)]
            self.wr_idx = 0
            fw.persist()
            fw.dma(fw.sp, g_pv.t[:], self.pvec, g_pv.res, w=[g_pv.res])
            fw.dma(fw.sp, g_id.t[:], self.ident, g_id.res, w=[g_id.res])
            fw.op(fw.pool, lambda e: e.memset(g_ones.t[:], 1.0), w=[g_ones.res])

            self.phase_mod()
            jobs = []
            for b in range(NB):
                jobs.append((self.x[b * L:(b + 1) * L, :], self.xt_view(b), L, self.r_XT[b]))
                jobs.append((self.ctx[b * CL:(b + 1) * CL, :], self.ct_view(b), CL, self.r_CT[b]))
            self.phase_transpose_in(jobs)
            for i in range(self.nlayers):
                self.layer(i)
            self.phase_transpose_out()
            fw.barrier()

    def xt_view(self, b):
        return self.XT.rearrange("(c p) t -> p c t", p=128)[:, :, b * L:(b + 1) * L]

    def ct_view(self, b):
        return self.CT.rearrange("(c p) t -> p c t", p=128)[:, :, b * CL:(b + 1) * CL]

    def ht_view(self, b, ctx):
        v = self.HT.rearrange("(c p) t -> p c t", p=128)
        o = b * (L + CL)
        return v[:, :, o + L:o + L + CL] if ctx else v[:, :, o:o + L]

    def pvc(self, name, col=0, n=1):
        o = PVL.off[name] + col
        return self.g_pv.t[:, o:o + n]

    def psw(self, b0, width):
        return self.ps[:, b0 * 512:b0 * 512 + width]

    def banks(self, b0, width):
        return self.bank[b0:b0 + (width + 511) // 512]

    def mm(self, out, lhsT, rhs, start, stop, r, w, sig=False):
        return self.fw.op(self.fw.pe, lambda e: e.matmul(out, lhsT, rhs, start=start, stop=stop), r=r, w=w, sig=sig)

    def tr(self, out, in_, r, w, sig=False):
        idt = self.g_id
        return self.fw.op(self.fw.pe, lambda e: e.transpose(out, in_, idt.t[:]), r=list(r) + [idt.res], w=w, sig=sig)

    def actf(self, out, in_, func, r, w=(), pw=(), bias=None, scale=None):
        kw = {}
        if bias is not None:
            kw["bias"] = bias
        if scale is not None:
            kw["scale"] = scale
        return self.fw.op(self.fw.act, lambda e: e.activation(out=out, in_=in_, func=func, **kw), r=r, w=w, pw=pw)

    def stt(self, out, in0, scalar, in1, op0, op1, r, w=(), pw=()):
        return self.fw.op(self.fw.dve, lambda e: e.scalar_tensor_tensor(out=out, in0=in0, scalar=scalar, in1=in1, op0=op0, op1=op1), r=r, w=w, pw=pw)

    def tt(self, out, in0, in1, op, r, w=(), pw=(), eng=None):
        return self.fw.op(eng or self.fw.dve, lambda e: e.tensor_tensor(out=out, in0=in0, in1=in1, op=op), r=r, w=w, pw=pw)

    def ts(self, out, in0, s1, s2, op0, op1, r, w=(), pw=(), eng=None):
        if s2 is None:
            return self.fw.op(eng or self.fw.dve, lambda e: e.tensor_scalar(out=out, in0=in0, scalar1=s1, scalar2=None, op0=op0), r=r, w=w, pw=pw)
        return self.fw.op(eng or self.fw.dve, lambda e: e.tensor_scalar(out=out, in0=in0, scalar1=s1, scalar2=s2, op0=op0, op1=op1), r=r, w=w, pw=pw)

    def cp(self, eng, out, in_, r, w=(), pw=()):
        fw = self.fw
        if eng is fw.act:
            return fw.op(eng, lambda e: e.activation(out=out, in_=in_, func=AF.Copy), r=r, w=w, pw=pw)
        return fw.op(eng, lambda e: e.tensor_copy(out, in_), r=r, w=w, pw=pw)

    def recip(self, out, in_, r, w=(), pw=()):
        return self.fw.op(self.fw.dve, lambda e: e.reciprocal(out, in_), r=r, w=w, pw=pw)

    def phase_mod(self):
        fw = self.fw
        g_mod, g_mA = self.g_mod, self.g_mA
        with ExitStack() as st:
            cs = fw.sb(st, "csil", [128, 16, 3], F32, dma=True)
            csb = fw.sb(st, "csilb", [128, 16, 3], BF16)
            fw.dma(fw.sp, cs.t[:], self.csil, cs.res, w=[cs.res])
            self.actf(csb.t[:], cs.t[:], AF.Silu, r=[cs.res], w=[csb.res])
            wb = [fw.sb(st, f"mw{k}", [128, 16, 512], BF16, dma=True) for k in range(3)]
            n = 0
            for i in range(self.nlayers):
                wv = self.mod_w[i].rearrange("(k p) n -> p k n", p=128)
                for piece in range(24):
                    buf = wb[n % 3]
                    bk = n % 2
                    n += 1
                    fw.dma(fw.pool, buf.t[:], wv[:, :, piece * 512:(piece + 1) * 512], buf.res, w=[buf.res])
                    for qq in range(4):
                        for k in range(16):
                            self.mm(self.ps[:, bk * 512 + qq * 3:bk * 512 + qq * 3 + 3],
                                    buf.t[:, k, qq * 128:(qq + 1) * 128], csb.t[:, k, :],
                                    start=(k == 0), stop=(k == 15), r=[buf.res, csb.res], w=[self.bank[bk]],
                                    sig=(k == 15 and qq == 3))
                    for qq in range(4):
                        q = piece * 4 + qq
                        self.ts(g_mod.t[:, i, q, :], self.ps[:, bk * 512 + qq * 3:bk * 512 + qq * 3 + 3],
                                self.pvc(f"modb{i}", q), None, ALU.add, None,
                                r=[self.bank[bk], self.g_pv.res], pw=[g_mod.res])
                for which in range(2):
                    for c in range(16):
                        sc = (16 if which == 0 else 64) + c
                        self.ts(g_mA.t[:, i, which, c, :], g_mod.t[:, i, sc, :], 1.0,
                                self.pvc(f"n{which + 1}g{i}", c), ALU.add, ALU.mult,
                                r=[g_mod.res, self.g_pv.res], pw=[g_mA.res])
            fw.barrier()

    def phase_transpose_in(self, jobs):
        fw = self.fw
        with ExitStack() as st:
            xin = [fw.sb(st, "xin", [128, 4, D], F32, dma=True) for _ in range(2)]
            stg = [fw.sb(st, "stg", [128, 16, 512], F32, dma=True) for _ in range(2)]
            tiles = [(src, dst, dres, t0, min(512, Lx - t0) // 128) for (src, dst, Lx, dres) in jobs for t0 in range(0, Lx, 512)]

            def ld_x(k):
                src, dst, dres, t0, nt = tiles[k]
                fw.dma(fw.sp, xin[k % 2].t[:, 0:nt, :], src[t0:t0 + nt * 128, :].rearrange("(n p) d -> p n d", p=128),
                       xin[k % 2].res, w=[xin[k % 2].res])

            ld_x(0)
            nb = 0
            for it, (src, dst, dres, t0, nt) in enumerate(tiles):
                xi, sg = xin[it % 2], stg[it % 2]
                if it + 1 < len(tiles):
                    ld_x(it + 1)
                for c in range(16):
                    bk = nb % 8
                    nb += 1
                    for n in range(nt):
                        self.tr(self.ps[:, bk * 512 + n * 128:bk * 512 + (n + 1) * 128],
                                xi.t[:, n, c * 128:(c + 1) * 128], r=[xi.res], w=[self.bank[bk]], sig=(n == nt - 1))
                    eng = fw.dve if c % 2 == 0 else fw.act
                    self.cp(eng, sg.t[:, c, 0:nt * 128], self.psw(bk, nt * 128), r=[self.bank[bk]], pw=[sg.res])
                fw.dma(fw.sp, dst[:, :, t0:t0 + nt * 128], sg.t[:, :, 0:nt * 128], sg.res, r=[sg.res], pw=dres)
            fw.barrier()

    def phase_transpose_out(self):
        fw = self.fw
        with ExitStack() as st:
            xin = [fw.sb(st, "xin", [128, 16, 512], F32, dma=True) for _ in range(2)]
            stg = [fw.sb(st, "stg", [128, 4, D], F32, dma=True) for _ in range(2)]
            tiles = [(b, t0) for b in range(NB) for t0 in range(0, L, 512)]

            def ld_o(k):
                b, t0 = tiles[k]
                fw.dma(fw.sp, xin[k % 2].t[:], self.xt_view(b)[:, :, t0:t0 + 512], xin[k % 2].res, r=self.r_XT[b], w=[xin[k % 2].res])

            ld_o(0)
            nb = 0
            for it, (b, t0) in enumerate(tiles):
                xi, sg = xin[it % 2], stg[it % 2]
                if it + 1 < len(tiles):
                    ld_o(it + 1)
                for n in range(4):
                    for cq in range(4):
                        bk = nb % 8
                        nb += 1
                        for cc in range(4):
                            c = cq * 4 + cc
                            self.tr(self.ps[:, bk * 512 + cc * 128:bk * 512 + (cc + 1) * 128],
                                    xi.t[:, c, n * 128:(n + 1) * 128], r=[xi.res], w=[self.bank[bk]], sig=(cc == 3))
                        eng = fw.dve if nb % 2 == 0 else fw.act
                        self.cp(eng, sg.t[:, n, cq * 512:(cq + 1) * 512], self.psw(bk, 512), r=[self.bank[bk]], pw=[sg.res])
                fw.dma(fw.sp, self.out[b * L + t0:b * L + t0 + 512, :].rearrange("(n p) d -> p n d", p=128),
                       sg.t[:], sg.res, r=[sg.res])
            fw.barrier()

    def phase_norm_mod(self, jobs, i, which):
        fw = self.fw
        g_mA, g_mod = self.g_mA, self.g_mod
        shb = 0 if which == 0 else 48
        with ExitStack() as st:
            xts = [fw.sb(st, "xt", [128, 16, 512], F32, dma=True) for _ in range(3)]
            sq = fw.sb(st, "sq", [128, 16, 512], BF16)
            sds = [fw.sb(st, "sd", [128, 512], F32) for _ in range(2)]
            rss = [fw.sb(st, "rs", [128, 512], F32) for _ in range(2)]
            tmps = [fw.sb(st, "tmp", [128, 512], F32) for _ in range(2)]
            hbs = [fw.sb(st, "hb", [128, 16, 512], BF16, dma=True) for _ in range(2)]
            tiles = [(src, sres, dst, dres, rr, t0, min(512, Lx - t0))
                     for (src, sres, dst, dres, Lx, rr) in jobs for t0 in range(0, Lx, 512)]
            nt_ = len(tiles)
            cnt = [0]

            def ld(k):
                src, sres, dst, dres, rr, t0, n = tiles[k]
                xt = xts[k % 3]
                fw.dma(fw.sp, xt.t[:, :, 0:n], src[:, :, t0:t0 + n], xt.res, r=sres, w=[xt.res])

            def s1(k):
                n = tiles[k][6]
                xt = xts[k % 3]
                bk = k % 8
                self.actf(sq.t[:, :, 0:n], xt.t[:, :, 0:n], AF.Square, r=[xt.res], w=[sq.res])
                for c in range(16):
                    self.mm(self.psw(bk, n), self.g_ones.t[:], sq.t[:, c, 0:n], start=(c == 0), stop=(c == 15),
                            r=[sq.res, self.g_ones.res], w=[self.bank[bk]], sig=(c == 15))

            def s2(k):
                n = tiles[k][6]
                bk = k % 8
                sd, rs = sds[k % 2], rss[k % 2]
                self.actf(sd.t[:, 0:n], self.psw(bk, n), AF.Sqrt, r=[self.bank[bk]], w=[sd.res], bias=EPS, scale=1.0 / D)
                self.recip(rs.t[:, 0:n], sd.t[:, 0:n], r=[sd.res], w=[rs.res])

            def ap(k):
                src, sres, dst, dres, rr, t0, n = tiles[k]
                xt, rs, hb = xts[k % 3], rss[k % 2], hbs[k % 2]
                for c in range(16):
                    tmp = tmps[cnt[0] % 2]
                    cnt[0] += 1
                    self.stt(tmp.t[:, 0:n], xt.t[:, c, 0:n], g_mA.t[:, i, which, c, rr:rr + 1], rs.t[:, 0:n],
                             ALU.mult, ALU.mult, r=[xt.res, rs.res, g_mA.res], w=[tmp.res])
                    self.actf(hb.t[:, c, 0:n], tmp.t[:, 0:n], AF.Identity, r=[tmp.res, g_mod.res], pw=[hb.res],
                              bias=g_mod.t[:, i, shb + c, rr:rr + 1])
                fw.dma(fw.sp, dst[:, :, t0:t0 + n], hb.t[:, :, 0:n], hb.res, r=[hb.res], pw=dres)

            ld(0)
            if nt_ > 1:
                ld(1)
            s1(0)
            s2(0)
            for k in range(nt_):
                if k + 2 < nt_:
                    ld(k + 2)
                if k + 1 < nt_:
                    s1(k + 1)
                ap(k)
                if k + 1 < nt_:
                    s2(k + 1)
            fw.barrier()

    def load_in(self, st, src, sres, KC, Lx, name="inT"):
        fw = self.fw
        t = fw.sb(st, name, [128, KC, Lx], BF16, dma=True)
        step = 4
        t.kres = []
        for k0 in range(0, KC, step):
            k1 = min(KC, k0 + step)
            gr = fw.res(f"{name}_g{k0}", dma=True)
            fw.dma(fw.sp, t.t[:, k0:k1, :], src[:, k0:k1, :], gr, r=sres, w=[gr])
            t.kres += [gr] * (k1 - k0)
        return t

    def linear(self, st, inT, KC, Lx, W, parts, MC, SUP, epi, nw=3):
        fw = self.fw
        P = len(parts)
        nbk = (Lx + 511) // 512
        per = P * nbk
        nsets = max(1, 8 // per)
        wv = W.rearrange("(k p) n -> p k n", p=128)
        assert KC * P * SUP * 128 <= WR_ELEMS
        for sup in range(MC // SUP):
            wr = self.wring[self.wr_idx % NWR]
            self.wr_idx += 1
            wb = T(wr.t[:, 0:KC * P * SUP * 128].rearrange("p (k q s) -> p k q s", k=KC, q=P), wr.res)
            for p in range(P):
                c0 = parts[p] + sup * SUP * 128
                fw.dma(fw.pool, wb.t[:, :, p, :], wv[:, :, c0:c0 + SUP * 128], wb.res, w=[wb.res] if p == 0 else (), pw=() if p == 0 else [wb.res])
            for mm_ in range(SUP):
                m = sup * SUP + mm_
                base = (m % nsets) * per
                for p in range(P):
                    b0 = base + p * nbk
                    for nt in range(nbk):
                        n0 = nt * 512
                        n = min(512, Lx - n0)
                        for k in range(KC):
                            self.mm(self.psw(b0 + nt, n), wb.t[:, k, p, mm_ * 128:(mm_ + 1) * 128], inT.t[:, k, n0:n0 + n],
                                    start=(k == 0), stop=(k == KC - 1),
                                    r=[wb.res, inT.kres[k] if inT.kres else inT.res], w=[self.bank[b0 + nt]],
                                    sig=(k == KC - 1))
                    epi(m, p, self.psw(b0, Lx), self.banks(b0, Lx))

    def resid_epi(self, st, xview, xres, Lx, i, gbase, rr, bias_name):
        fw = self.fw
        g_mod = self.g_mod
        xin = [fw.sb(st, "xi", [128, Lx], F32, dma=True) for _ in range(3)]
        xo = [fw.sb(st, "xo", [128, Lx], F32, dma=True) for _ in range(2)]
        gb = None
        if bias_name is not None:
            gb = fw.sb(st, "gb", [128, 16], F32)
            self.tt(gb.t[:, :], g_mod.t[:, i, gbase:gbase + 16, rr], self.pvc(bias_name, 0, 16), ALU.mult,
                    r=[g_mod.res, self.g_pv.res], w=[gb.res])
        cnt = [0]

        def epi(m, p, ps, bks):
            xi = xin[cnt[0] % 3]
            o = xo[cnt[0] % 2]
            cnt[0] += 1
            fw.dma(fw.sp, xi.t[:], xview[:, m, :], xi.res, r=[xres[m]], w=[xi.res])
            self.stt(o.t[:], ps, g_mod.t[:, i, gbase + m, rr:rr + 1], xi.t[:], ALU.mult, ALU.add,
                     r=list(bks) + [xi.res, g_mod.res], w=[o.res])
            if gb is not None:
                self.actf(o.t[:], o.t[:], AF.Identity, r=[o.res, gb.res], w=[o.res], bias=gb.t[:, m:m + 1])
            fw.dma(fw.sp, xview[:, m, :], o.t[:], o.res, r=[o.res], pw=[xres[m]])
        return epi

    def scr_view(self, name, Lx, t0=0):
        return getattr(self, name).rearrange("(c p) t -> p c t", p=128)[:, :, t0:t0 + Lx]

    def conformer(self, i, j, hview, hres, xview, xres, Lx, rr, nseq=1):
        fw = self.fw
        UTv, VTv = self.scr_view("UT", Lx), self.scr_view("VT", Lx)
        rU, rV = self.r_scr["UT"], self.r_scr["VT"]
        with ExitStack() as st:
            inT = self.load_in(st, hview, [hres], 16, Lx)
            ta = [fw.sb(st, "ta", [128, Lx], F32) for _ in range(2)]
            sg = [fw.sb(st, "sg", [128, Lx], F32) for _ in range(2)]
            uo = [fw.sb(st, "uo", [128, Lx], F32, dma=True) for _ in range(2)]

            def epi(m, p, ps, bks):
                if p == 0:
                    self.actf(ta[m % 2].t[:], ps, AF.Identity, r=list(bks) + [self.g_pv.res], w=[ta[m % 2].res],
                              bias=self.pvc(f"abpw1{j}", m))
                else:
                    self.actf(sg[m % 2].t[:], ps, AF.Sigmoid, r=list(bks) + [self.g_pv.res], w=[sg[m % 2].res],
                              bias=self.pvc(f"abpw1{j}", 16 + m))
                    o = uo[m % 2]
                    self.tt(o.t[:], ta[m % 2].t[:], sg[m % 2].t[:], ALU.mult, r=[ta[m % 2].res, sg[m % 2].res], w=[o.res])
                    fw.dma(fw.sp, UTv[:, m, :], o.t[:], o.res, r=[o.res], pw=[rU])
            self.linear(st, inT, 16, Lx, self.a_w_pw1[j], [0, D], 16, 2, epi)
            fw.barrier()
        with ExitStack() as so:
            mean = fw.sb(so, "mean", [128, Lx], F32)
            rstd = fw.sb(so, "rstd", [128, Lx], F32)
            nbk = (Lx + 511) // 512
            with ExitStack() as st:
                ubf = [fw.sb(st, "ubf", [128, Lx], F32, dma=True) for _ in range(2)]
                Ls = Lx // nseq
                ubp = [fw.sb(st, "ubp", [128, nseq, Ls + 30], BF16) for _ in range(2)]
                dg = [fw.sb(st, "dg", [128, 31, 128], BF16) for _ in range(2)]
                idb = fw.sb(st, "idb", [128, 128], BF16)
                acc = [fw.sb(st, "acc", [128, Lx], F32, dma=True) for _ in range(2)]
                vb = [fw.sb(st, "vb", [128, 512], BF16) for _ in range(3)]
                vq = [fw.sb(st, "vq", [128, 512], BF16) for _ in range(3)]
                sacc = fw.sb(st, "sacc", [128, Lx], F32)
                qacc = fw.sb(st, "qacc", [128, Lx], F32)
                self.cp(fw.dve, idb.t[:], self.g_id.t[:], r=[self.g_id.res], w=[idb.res])
                for u in ubp:
                    fw.op(fw.dve, lambda e, u=u: e.memset(u.t[:, :, 0:15], 0.0), pw=[u.res])
                    fw.op(fw.dve, lambda e, u=u: e.memset(u.t[:, :, Ls + 15:Ls + 30], 0.0), pw=[u.res])
                if nseq == 1:
                    ctiles = [(n0, min(512, Lx - n0), [(0, n0, min(512, Lx - n0), 0)]) for n0 in range(0, Lx, 512)]
                else:
                    assert nseq * Ls <= 512
                    ctiles = [(0, Lx, [(s_, 0, Ls, s_ * Ls) for s_ in range(nseq)])]
                state = {"ti": 0, "si": 0, "pend": None}

                def emit_stats(item):
                    c, n0, n, vb_, vq_ = item
                    s_ = state["si"] % 2
                    state["si"] += 1
                    self.mm(self.psw(4 + s_, n), self.g_ones.t[:], vb_.t[:, 0:n], start=True, stop=True,
                            r=[vb_.res, self.g_ones.res], w=[self.bank[4 + s_]], sig=True)
                    self.mm(self.psw(6 + s_, n), self.g_ones.t[:], vq_.t[:, 0:n], start=True, stop=True,
                            r=[vq_.res, self.g_ones.res], w=[self.bank[6 + s_]], sig=True)
                    if c == 0:
                        self.cp(fw.dve, sacc.t[:, n0:n0 + n], self.psw(4 + s_, n), r=[self.bank[4 + s_]], pw=[sacc.res])
                        self.cp(fw.dve, qacc.t[:, n0:n0 + n], self.psw(6 + s_, n), r=[self.bank[6 + s_]], pw=[qacc.res])
                    else:
                        self.tt(sacc.t[:, n0:n0 + n], self.psw(4 + s_, n), sacc.t[:, n0:n0 + n], ALU.add,
                                r=[self.bank[4 + s_], sacc.res], w=[sacc.res])
                        self.tt(qacc.t[:, n0:n0 + n], self.psw(6 + s_, n), qacc.t[:, n0:n0 + n], ALU.add,
                                r=[self.bank[6 + s_], qacc.res], w=[qacc.res])

                def c2_load(c):
                    fw.dma(fw.sp, ubf[c % 2].t[:], UTv[:, c, :], ubf[c % 2].res, r=[rU], w=[ubf[c % 2].res])

                def c2_prep(c):
                    self.cp(fw.act, ubp[c % 2].t[:, :, 15:15 + Ls], ubf[c % 2].t[:].rearrange("p (s l) -> p s l", s=nseq),
                            r=[ubf[c % 2].res], w=[ubp[c % 2].res])
                    for k in range(31):
                        self.ts(dg[c % 2].t[:, k, :], idb.t[:], self.pvc(f"awdw{j}", k * 16 + c), None, ALU.mult, None,
                                r=[idb.res, self.g_pv.res], pw=[dg[c % 2].res])

                c2_load(0)
                c2_prep(0)
                for c in range(16):
                    uf, up, d_, a = ubf[c % 2], ubp[c % 2], dg[c % 2], acc[c % 2]
                    if c + 1 < 16:
                        c2_load(c + 1)
                    bdw = self.pvc(f"abdw{j}", c)
                    for nt, (n0, n, pieces) in enumerate(ctiles):
                        bk = state["ti"] % 4
                        vb_, vq_ = vb[state["ti"] % 3], vq[state["ti"] % 3]
                        state["ti"] += 1
                        for (s_, o_, ln_, pc_) in pieces:
                            for k in range(31):
                                self.mm(self.ps[:, bk * 512 + pc_:bk * 512 + pc_ + ln_], d_.t[:, k, :],
                                        up.t[:, s_, k + o_:k + o_ + ln_], start=(k == 0), stop=(k == 30),
                                        r=[d_.res, up.res], w=[self.bank[bk]], sig=(k == 30))
                        rr_ = [self.bank[bk], self.g_pv.res]
                        self.actf(a.t[:, n0:n0 + n], self.psw(bk, n), AF.Identity, r=rr_, pw=[a.res], bias=bdw)
                        self.actf(vb_.t[:, 0:n], self.psw(bk, n), AF.Identity, r=rr_, w=[vb_.res], bias=bdw)
                        self.actf(vq_.t[:, 0:n], self.psw(bk, n), AF.Square, r=rr_, w=[vq_.res], bias=bdw)
                        if state["pend"] is not None:
                            emit_stats(state["pend"])
                        state["pend"] = (c, n0, n, vb_, vq_)
                        if c + 1 < 16 and nt == min(1, len(ctiles) - 1):
                            c2_prep(c + 1)
                    fw.dma(fw.sp, VTv[:, c, :], a.t[:], a.res, r=[a.res], pw=[rV])
                emit_stats(state["pend"])
                msq = fw.sb(st, "msq", [128, Lx], F32)
                var = fw.sb(st, "var", [128, Lx], F32)
                self.actf(mean.t[:], sacc.t[:], AF.Identity, r=[sacc.res], w=[mean.res], scale=1.0 / D)
                self.tt(msq.t[:], mean.t[:], mean.t[:], ALU.mult, r=[mean.res], w=[msq.res])
                self.stt(var.t[:], qacc.t[:], 1.0 / D, msq.t[:], ALU.mult, ALU.subtract,
                         r=[qacc.res, msq.res], w=[var.res])
                self.actf(msq.t[:], var.t[:], AF.Sqrt, r=[var.res], w=[msq.res], bias=EPS)
                self.recip(rstd.t[:], msq.t[:], r=[msq.res], w=[rstd.res])
                fw.barrier()
            with ExitStack() as st:
                zT = fw.sb(st, "zT", [128, 16, Lx], BF16)
                zT.kres = [fw.res(f"z{c}") for c in range(16)]
                vin = [fw.sb(st, "vin", [128, Lx], F32, dma=True) for _ in range(2)]
                t1 = [fw.sb(st, "t1", [128, Lx], F32) for _ in range(2)]
                for c in range(16):
                    v = vin[c % 2]
                    t = t1[c % 2]
                    fw.dma(fw.sp, v.t[:], VTv[:, c, :], v.res, r=[rV], w=[v.res])
                    self.tt(t.t[:], v.t[:], mean.t[:], ALU.subtract, r=[v.res, mean.res], w=[t.res])
                    self.tt(t.t[:], t.t[:], rstd.t[:], ALU.mult, r=[t.res, rstd.res], w=[t.res])
                    self.actf(zT.t[:, c, :], t.t[:], AF.Silu, r=[t.res, self.g_pv.res], w=[zT.kres[c]],
                              bias=self.pvc(f"alnb{j}", c), scale=self.pvc(f"alng{j}", c))
                epi = self.resid_epi(st, xview, xres, Lx, i, 32, rr, f"abpw2{j}")
                self.linear(st, zT, 16, Lx, self.a_w_pw2[j], [0], 16, 2, epi)
                fw.barrier()

    def ffn(self, i, hview, hres, xview, xres, Lx, rr, nseq=1):
        fw = self.fw
        GTv = self.scr_view("GT", Lx)
        rG = self.r_scr["GT"]
        with ExitStack() as st:
            inT = self.load_in(st, hview, [hres], 16, Lx)
            yv = [fw.sb(st, "yv", [128, Lx], F32) for _ in range(2)]
            yg = [fw.sb(st, "yg", [128, Lx], F32) for _ in range(2)]
            go = [fw.sb(st, "go", [128, Lx], BF16, dma=True) for _ in range(2)]

            def epi(m, p, ps, bks):
                ch = m + 44 * p
                y = (yv if p == 0 else yg)[m % 2]
                rs_ = list(bks) + [self.g_pv.res]
                self.actf(y.t[:], ps, AF.Identity, r=rs_, w=[y.res], bias=self.pvc(f"fbdw{i}", ch),
                          scale=self.pvc(f"fwdw{i}", 88 + ch))
                Ls = Lx // nseq
                for s_ in range(nseq):
                    a_, b_ = s_ * Ls, (s_ + 1) * Ls
                    self.stt(y.t[:, a_ + 1:b_], ps[:, a_:b_ - 1], self.pvc(f"fwdw{i}", ch), y.t[:, a_ + 1:b_], ALU.mult, ALU.add,
                             r=rs_ + [y.res], w=[y.res])
                    self.stt(y.t[:, a_:b_ - 1], ps[:, a_ + 1:b_], self.pvc(f"fwdw{i}", 176 + ch), y.t[:, a_:b_ - 1], ALU.mult, ALU.add,
                             r=rs_ + [y.res], w=[y.res])
                if p == 1:
                    self.actf(y.t[:], y.t[:], AF.Silu, r=[y.res], w=[y.res])
                    o = go[m % 2]
                    self.tt(o.t[:], y.t[:], yv[m % 2].t[:], ALU.mult, r=[y.res, yv[m % 2].res], w=[o.res])
                    fw.dma(fw.sp, GTv[:, m, :], o.t[:], o.res, r=[o.res], pw=[rG])
            self.linear(st, inT, 16, Lx, self.f_w_up[i], [0, DFF], 44, 2, epi)
            fw.barrier()
        Lh = min(1024, Lx)
        for h0 in range(0, Lx, Lh):
            with ExitStack() as st:
                inT = self.load_in(st, self.scr_view("GT", Lh, h0), [rG], 44, Lh)
                epi = self.resid_epi(st, xview[:, :, h0:h0 + Lh], xres, Lh, i, 80, rr, None)
                self.linear(st, inT, 44, Lh, self.f_w_down[i], [0], 16, 1, epi)
                fw.barrier()

    def ctm_view(self):
        return self.CT.rearrange("(c p) t -> p c t", p=128)

    def htc_view(self):
        return self.HTC.rearrange("(c p) t -> p c t", p=128)

    def layer(self, i):
        kind = i % 3
        j = i // 3
        ctx_out = any(l % 3 == 2 for l in range(i + 1, DEPTH))
        ctx_in = ctx_out or kind == 2
        upto = self.upto
        CM = NB * CL
        lat = [(self.xt_view(b), self.r_XT[b], L, b, self.ht_view(b, False), self.r_HT[b]) for b in range(NB)]
        ctm = (self.ctm_view(), self.r_CT[0], CM, 2, self.htc_view(), self.r_HTC)
        jobs = [(xv, xr, hv, [hr], Lx, rr) for (xv, xr, Lx, rr, hv, hr) in lat]
        if ctx_in:
            if kind == 2:
                jobs += [(self.ct_view(b), self.r_CT[b], self.ht_view(b, True), [self.r_HT[b]], CL, 2) for b in range(NB)]
            else:
                jobs += [(ctm[0], ctm[1], ctm[4], [ctm[5]], CM, 2)]
        self.phase_norm_mod(jobs, i, 0)
        if upto == (i, "norm1"):
            return
        if kind == 0:
            for (xv, xr, Lx, rr, hv, hr) in lat:
                self.conformer(i, j, hv, hr, xv, xr, Lx, rr)
            if ctx_out:
                self.conformer(i, j, ctm[4], ctm[5], ctm[0], ctm[1], CM, 2, nseq=NB)
        elif kind == 1:
            for (xv, xr, Lx, rr, hv, hr) in lat:
                self.fnet(i, j, hv, hr, xv, xr, Lx, rr, 0)
            if ctx_out:
                self.fnet(i, j, ctm[4], ctm[5], ctm[0], ctm[1], CM, 2, 1, nseq=NB)
        else:
            for b in range(NB):
                self.attention(i, j, b)
        if upto == (i, "mixer"):
            return
        jobs = [(xv, xr, hv, [hr], Lx, rr) for (xv, xr, Lx, rr, hv, hr) in lat]
        if ctx_out:
            jobs += [(ctm[0], ctm[1], ctm[4], [ctm[5]], CM, 2)]
        self.phase_norm_mod(jobs, i, 1)
        for (xv, xr, Lx, rr, hv, hr) in lat:
            self.ffn(i, hv, hr, xv, xr, Lx, rr)
        if ctx_out:
            self.ffn(i, ctm[4], ctm[5], ctm[0], ctm[1], CM, 2, nseq=NB)

    def fnet(self, i, j, hview, hres, xview, xres, Lx, rr, v, nseq=1):
        fw = self.fw
        rAB, rY = self.r_scr["AB"], self.r_scr["YT"]
        LC = Lx // 128
        with ExitStack() as st:
            inT = self.load_in(st, hview, [hres], 16, Lx)
            cs = fw.sb(st, "cs", [128, 2, 512], BF16, dma=True)
            fw.dma(fw.sp, cs.t[:], self.dft_cs[v], cs.res, w=[cs.res])
            abo = [fw.sb(st, "abo", [128, 8, 512], BF16, dma=True) for _ in range(2)]
            nb = 0
            for tc in range(LC):
                o = abo[tc % 2]
                for g in range(8):
                    bk = nb % 8
                    nb += 1
                    for kk in range(2):
                        self.mm(self.psw(bk, 512), inT.t[:, 2 * g + kk, tc * 128:(tc + 1) * 128], cs.t[:, kk, :],
                                start=(kk == 0), stop=(kk == 1), r=[inT.kres[2 * g + kk], cs.res], w=[self.bank[bk]], sig=(kk == 1))
                    self.cp(fw.dve if g % 2 == 0 else fw.act, o.t[:, g, :], self.psw(bk, 512), r=[self.bank[bk]], pw=[o.res])
                fw.dma(fw.sp, self.AB[tc * 128:(tc + 1) * 128, :].rearrange("p (g n) -> p g n", g=8), o.t[:], o.res,
                       r=[o.res], pw=[rAB])
            fw.barrier()
        if nseq > 1:
            Ls = Lx // nseq
            LCs = Ls // 128
            with ExitStack() as st:
                cl = fw.sb(st, "cl", [128, LCs, Ls], BF16, dma=True)
                sl = fw.sb(st, "sl", [128, LCs, Ls], BF16, dma=True)
                srcm = self.dft_c
                fw.dma(fw.sp, cl.t[:], srcm[0], cl.res, w=[cl.res])
                fw.dma(fw.sp, sl.t[:], srcm[1], sl.res, w=[sl.res])
                abg = [fw.sb(st, "abg", [128, LC, 512], BF16, dma=True) for _ in range(2)]
                yo = [fw.sb(st, "yo", [128, Lx], BF16, dma=True) for _ in range(2)]
                ABv = self.AB.rearrange("(lc p) n -> p lc n", p=128)
                YTv = self.scr_view("YT", Lx)

                def ld_ab2(g):
                    fw.dma(fw.sp, abg[g % 2].t[:], ABv[:, 0:LC, g * 512:(g + 1) * 512], abg[g % 2].res, r=[rAB], w=[abg[g % 2].res])

                ld_ab2(0)
                for g in range(8):
                    a = abg[g % 2]
                    if g + 1 < 8:
                        ld_ab2(g + 1)
                    for hf in range(2):
                        m = 2 * g + hf
                        bk = m % 8
                        for s_ in range(nseq):
                            pcols = self.ps[:, bk * 512 + s_ * Ls:bk * 512 + (s_ + 1) * Ls]
                            for lc in range(LCs):
                                self.mm(pcols, a.t[:, s_ * LCs + lc, hf * 128:(hf + 1) * 128], cl.t[:, lc, :],
                                        start=(lc == 0), stop=False, r=[a.res, cl.res], w=[self.bank[bk]])
                                self.mm(pcols, a.t[:, s_ * LCs + lc, 256 + hf * 128:256 + (hf + 1) * 128], sl.t[:, lc, :],
                                        start=False, stop=(lc == LCs - 1), r=[a.res, sl.res], w=[self.bank[bk]],
                                        sig=(lc == LCs - 1))
                        o = yo[m % 2]
                        self.cp(fw.dve if m % 2 == 0 else fw.act, o.t[:], self.psw(bk, Lx), r=[self.bank[bk]], w=[o.res])
                        fw.dma(fw.sp, YTv[:, m, :], o.t[:], o.res, r=[o.res], pw=[rY])
                fw.barrier()
        if nseq == 1:
            with ExitStack() as st:
                nh = 2 if Lx >= 1024 else 1
                Lh = Lx // nh
                cl = fw.sb(st, "cl", [128, LC, Lh], BF16, dma=True)
                sl = fw.sb(st, "sl", [128, LC, Lh], BF16, dma=True)
                srcm = self.dft_l if v == 0 else self.dft_c
                abg = [fw.sb(st, "abg", [128, LC, 512], BF16, dma=True) for _ in range(2)]
                yo = [fw.sb(st, "yo", [128, Lh], BF16, dma=True) for _ in range(2)]
                ABv = self.AB.rearrange("(lc p) n -> p lc n", p=128)
                YTv = self.scr_view("YT", Lx)
                nbk = (Lh + 511) // 512
                seq = [(h, g) for h in range(nh) for g in range(8)]

                def ld_ab(k):
                    g = seq[k][1]
                    fw.dma(fw.sp, abg[k % 2].t[:], ABv[:, 0:LC, g * 512:(g + 1) * 512], abg[k % 2].res, r=[rAB], w=[abg[k % 2].res])

                def ld_dft(h):
                    for k0 in range(0, LC, 4):
                        k1 = min(LC, k0 + 4)
                        fw.dma(fw.sp, cl.t[:, k0:k1, :], srcm[0][:, k0:k1, h * Lh:(h + 1) * Lh], cl.res, pw=[cl.res])
                        fw.dma(fw.sp, sl.t[:, k0:k1, :], srcm[1][:, k0:k1, h * Lh:(h + 1) * Lh], sl.res, pw=[sl.res])

                ld_ab(0)
                cnt = 0
                for k, (h, g) in enumerate(seq):
                    if g == 0:
                        ld_dft(h)
                    a = abg[k % 2]
                    if k + 1 < len(seq):
                        ld_ab(k + 1)
                    for hf in range(2):
                        m = 2 * g + hf
                        b0 = (cnt % (8 // nbk)) * nbk
                        o = yo[cnt % 2]
                        cnt += 1
                        for nt in range(nbk):
                            n0 = nt * 512
                            n = min(512, Lh - n0)
                            for lc in range(LC):
                                self.mm(self.psw(b0 + nt, n), a.t[:, lc, hf * 128:(hf + 1) * 128], cl.t[:, lc, n0:n0 + n],
                                        start=(lc == 0), stop=False, r=[a.res, cl.res], w=[self.bank[b0 + nt]])
                                self.mm(self.psw(b0 + nt, n), a.t[:, lc, 256 + hf * 128:256 + (hf + 1) * 128], sl.t[:, lc, n0:n0 + n],
                                        start=False, stop=(lc == LC - 1), r=[a.res, sl.res], w=[self.bank[b0 + nt]],
                                        sig=(lc == LC - 1))
                        self.cp(fw.dve if m % 2 == 0 else fw.act, o.t[:], self.psw(b0, Lh), r=self.banks(b0, Lh), w=[o.res])
                        fw.dma(fw.sp, YTv[:, m, h * Lh:(h + 1) * Lh], o.t[:], o.res, r=[o.res], pw=[rY])
                fw.barrier()
        with ExitStack() as st:
            inT = self.load_in(st, self.scr_view("YT", Lx), [rY], 16, Lx)
            epi = self.resid_epi(st, xview, xres, Lx, i, 32, rr, "bbout")
            self.linear(st, inT, 16, Lx, self.b_w_out[j], [0], 16, 2, epi)
            fw.barrier()

    def attention(self, i, j, b):
        fw = self.fw
        LT = L + CL
        hv = self.HT.rearrange("(c p) t -> p c t", p=128)[:, :, b * LT:(b + 1) * LT]
        hres = [self.r_HT[b]]
        rQ, rK, rVS, rO = (self.r_scr[n] for n in ("QT", "KT", "VS", "OT"))
        QTv = self.scr_view("QT", L)
        KTv = self.scr_view("KT", LT)
        wq = self.c_w_qkv[j].rearrange("(k p) n -> p k n", p=128)
        with ExitStack() as st:
            inT = self.load_in(st, hv, hres, 16, LT)
            wcur = {}
            epsq = fw.sb(st, "epsq", [128, 2], F32)
            fw.op(fw.dve, lambda e: e.memset(epsq.t[:, 0:1], HD * EPS), pw=[epsq.res])
            fw.op(fw.dve, lambda e: e.memset(epsq.t[:, 1:2], EPS), pw=[epsq.res])
            sq = [fw.sb(st, "sq", [128, 512], BF16) for _ in range(4)]
            sd = [fw.sb(st, "sd", [128, 512], F32) for _ in range(4)]
            rs = [fw.sb(st, "rs", [128, 512], F32) for _ in range(4)]
            qo = [fw.sb(st, "qo", [128, 512], BF16, dma=True) for _ in range(4)]
            items = []
            for m in range(32):
                tiles = [(t_, 512) for t_ in range(0, L, 512)] + ([(2048, 256)] if m >= 16 else [])
                for (t0, n) in tiles:
                    items.append((m, t0, n))

            def emit_main(idx):
                m, t0, n = items[idx]
                s = idx % 4
                if m % 2 == 0 and t0 == 0:
                    wr = self.wring[self.wr_idx % NWR]
                    self.wr_idx += 1
                    wcur[m // 2] = T(wr.t[:, 0:16 * 256].rearrange("p (k s) -> p k s", k=16), wr.res)
                    fw.dma(fw.pool, wcur[m // 2].t[:], wq[:, :, m * 128:(m + 2) * 128], wr.res, w=[wr.res])
                wb = wcur[m // 2]
                for k in range(16):
                    self.mm(self.psw(s * 2, n), wb.t[:, k, (m % 2) * 128:(m % 2 + 1) * 128],
                            inT.t[:, k, t0:t0 + n], start=(k == 0), stop=(k == 15),
                            r=[wb.res, inT.kres[k]], w=[self.bank[s * 2]], sig=(k == 15))
                self.actf(sq[s].t[:, 0:n], self.psw(s * 2, n), AF.Square, r=[self.bank[s * 2]], w=[sq[s].res])

            def emit_stat(idx):
                m, t0, n = items[idx]
                s = idx % 4
                isq = m < 16
                self.mm(self.psw(s * 2 + 1, n), self.g_ones.t[:], sq[s].t[:, 0:n], start=True, stop=True,
                        r=[sq[s].res, self.g_ones.res], w=[self.bank[s * 2 + 1]], sig=True)
                if isq:
                    self.actf(sd[s].t[:, 0:n], self.psw(s * 2 + 1, n), AF.Ln, r=[self.bank[s * 2 + 1], epsq.res], w=[sd[s].res],
                              bias=epsq.t[:, 0:1], scale=1.0)
                else:
                    self.actf(sd[s].t[:, 0:n], self.psw(s * 2 + 1, n), AF.Ln, r=[self.bank[s * 2 + 1], epsq.res], w=[sd[s].res],
                              bias=epsq.t[:, 1:2], scale=1.0 / HD)
                self.actf(rs[s].t[:, 0:n], sd[s].t[:, 0:n], AF.Exp, r=[sd[s].res], w=[rs[s].res], scale=-0.5)
                self.stt(qo[s].t[:, 0:n], self.psw(s * 2, n), self.pvc("cqg" if isq else "ckg"), rs[s].t[:, 0:n],
                         ALU.mult, ALU.mult, r=[self.bank[s * 2], rs[s].res, self.g_pv.res], w=[qo[s].res])
                if isq:
                    fw.dma(fw.sp, QTv[:, m, t0:t0 + n], qo[s].t[:, 0:n], qo[s].res, r=[qo[s].res], pw=[rQ])
                else:
                    fw.dma(fw.sp, KTv[:, m - 16, t0:t0 + n], qo[s].t[:, 0:n], qo[s].res, r=[qo[s].res], pw=[rK])

            for idx in range(len(items)):
                emit_main(idx)
                if idx > 0:
                    emit_stat(idx - 1)
            emit_stat(len(items) - 1)
            vo = [fw.sb(st, "vo", [128, 512], BF16, dma=True) for _ in range(4)]
            nb = 0
            for nt in range(4):
                wr = self.wring[self.wr_idx % NWR]
                self.wr_idx += 1
                wv = T(wr.t[:, 0:16 * 512].rearrange("p (k s) -> p k s", k=16), wr.res)
                fw.dma(fw.pool, wv.t[:], wq[:, :, 2 * D + nt * 512:2 * D + (nt + 1) * 512], wr.res, w=[wr.res])
                for tc in range(LT // 128):
                    bk = nb % 8
                    o = vo[nb % 4]
                    nb += 1
                    for k in range(16):
                        self.mm(self.psw(bk, 512), inT.t[:, k, tc * 128:(tc + 1) * 128], wv.t[:, k, :],
                                start=(k == 0), stop=(k == 15), r=[inT.kres[k], wv.res], w=[self.bank[bk]], sig=(k == 15))
                    self.cp(fw.dve if nb % 2 == 0 else fw.act, o.t[:], self.psw(bk, 512), r=[self.bank[bk]], w=[o.res])
                    fw.dma(fw.sp, self.VS[tc * 128:(tc + 1) * 128, nt * 512:(nt + 1) * 512], o.t[:], o.res, r=[o.res], pw=[rVS])
            fw.barrier()
        with ExitStack() as st:
            qh = [fw.sb(st, "qh", [128, L], BF16, dma=True) for _ in range(2)]
            kh = [fw.sb(st, "kh", [128, LT], BF16, dma=True) for _ in range(2)]
            vh = [fw.sb(st, "vh", [128, LT // 128, 128], BF16, dma=True) for _ in range(2)]
            bt = [fw.sb(st, "bt", [128, N_BT * 512], F32, dma=True) for _ in range(2)]
            sbf = [fw.sb(st, "sbf", [128, 512], F32) for _ in range(3)]
            pb = [fw.sb(st, "pb", [128, 512], BF16) for _ in range(9)]
            rc = [fw.sb(st, "rc", [128, 512], F32) for _ in range(2)]
            lnb = [fw.sb(st, "lnb", [128, 512], F32) for _ in range(2)]
            oo = [fw.sb(st, "oo", [128, 512], BF16, dma=True) for _ in range(2)]
            VSv = self.VS.rearrange("(tc p) d -> p tc d", p=128)
            OTv = self.scr_view("OT", L)
            cnt_s = 0
            cnt_p = 0
            grp = 0
            def load_qkb(h):
                q_, k_, b_ = qh[h % 2], kh[h % 2], bt[h % 2]
                fw.dma(fw.sp, q_.t[:], QTv[:, h, :], q_.res, r=[rQ], w=[q_.res])
                fw.dma(fw.sp, k_.t[:], KTv[:, h, :], k_.res, r=[rK], w=[k_.res])
                fw.dma(fw.sp, b_.t[:], self.att_bias[h], b_.res, w=[b_.res])

            def load_v(h):
                v_ = vh[h % 2]
                fw.dma(fw.sp, v_.t[:], VSv[:, :, h * 128:(h + 1) * 128], v_.res, r=[rVS], w=[v_.res])

            def load_head(h):
                load_qkb(h)
                load_v(h)

            items = []
            for h in range(NH):
                for g in range(4):
                    chunks = [(kc, ATT_TILE0[g] + ci) for ci, kc in enumerate(ATT_CHUNKS[g])] + [(16, None), (17, None)]
                    for ci, (kc, tile) in enumerate(chunks):
                        items.append((h, g, kc, tile, ci == 0, ci == len(chunks) - 1))
            LAG = 6
            NPB = len(pb)
            load_head(0)
            load_head(1)
            for idx in range(len(items) + LAG):
                if idx < len(items):
                    h, g, kc, tile, first, last = items[idx]
                    q_, k_, b_ = qh[h % 2], kh[h % 2], bt[h % 2]
                    sbk = idx % 4
                    self.mm(self.psw(sbk, 512), k_.t[:, kc * 128:(kc + 1) * 128], q_.t[:, g * 512:(g + 1) * 512],
                            start=True, stop=True, r=[k_.res, q_.res], w=[self.bank[sbk]], sig=True)
                    p_ = pb[idx % NPB]
                    if tile is not None:
                        sf = sbf[idx % 3]
                        self.tt(sf.t[:], self.psw(sbk, 512), b_.t[:, tile * 512:(tile + 1) * 512], ALU.add,
                                r=[self.bank[sbk], b_.res], w=[sf.res])
                        self.actf(p_.t[:], sf.t[:], AF.Exp, r=[sf.res], w=[p_.res])
                    else:
                        self.actf(p_.t[:], self.psw(sbk, 512), AF.Exp, r=[self.bank[sbk]], w=[p_.res])
                    if last and g == 3 and h + 2 < NH:
                        load_qkb(h + 2)
                if idx >= LAG:
                    jn = idx - LAG
                    h, g, kc, tile, first, last = items[jn]
                    v_ = vh[h % 2]
                    p_ = pb[jn % NPB]
                    grp = h * 4 + g
                    ob = 4 + (grp % 2)
                    db = 6 + (grp % 2)
                    self.mm(self.psw(ob, 512), v_.t[:, kc, :], p_.t[:], start=first, stop=last,
                            r=[v_.res, p_.res], w=[self.bank[ob]], sig=last)
                    self.mm(self.psw(db, 512), self.g_ones.t[:], p_.t[:], start=first, stop=last,
                            r=[self.g_ones.res, p_.res], w=[self.bank[db]], sig=last)
                    if last:
                        r_, o_, l_ = rc[grp % 2], oo[grp % 2], lnb[grp % 2]
                        self.actf(l_.t[:], self.psw(db, 512), AF.Ln, r=[self.bank[db]], w=[l_.res])
                        self.actf(r_.t[:], l_.t[:], AF.Exp, r=[l_.res], w=[r_.res], scale=-1.0)
                        self.tt(o_.t[:], self.psw(ob, 512), r_.t[:], ALU.mult, r=[self.bank[ob], r_.res], w=[o_.res])
                        fw.dma(fw.sp, OTv[:, h, g * 512:(g + 1) * 512], o_.t[:], o_.res, r=[o_.res], pw=[rO])
                        if g == 3 and h + 2 < NH:
                            load_v(h + 2)
            fw.barrier()
        with ExitStack() as st:
            inT = self.load_in(st, self.scr_view("OT", L), [rO], 16, L)
            epi = self.resid_epi(st, self.xt_view(b), self.r_XT[b], L, i, 32, b, None)
            self.linear(st, inT, 16, L, self.c_w_o[j], [0], 16, 2, epi)
            fw.barrier()


def fm(v):
    return np.ascontiguousarray(np.asarray(v, np.float32).reshape(-1, 128).T)


def build_pvec(inp):
    pv = np.zeros((128, PVL.n), np.float32)

    def put(name, a):
        o = PVL.off[name]
        pv[:, o:o + a.shape[1]] = a

    for i in range(DEPTH):
        put(f"n1g{i}", fm(inp["norm1_g"][i]))
        put(f"n2g{i}", fm(inp["norm2_g"][i]))
        put(f"modb{i}", fm(inp["mod_b"][i]))
        put(f"fwdw{i}", np.concatenate([fm(inp["f_w_dw"][i][k]) for k in range(3)], axis=1))
        put(f"fbdw{i}", fm(inp["f_b_dw"][i]))
    for j in range(2):
        put(f"abpw1{j}", fm(inp["a_b_pw1"][j]))
        put(f"awdw{j}", np.concatenate([fm(inp["a_w_dw"][j][k]) for k in range(31)], axis=1))
        put(f"abdw{j}", fm(inp["a_b_dw"][j]))
        put(f"alng{j}", fm(inp["a_ln_g"][j]))
        put(f"alnb{j}", fm(inp["a_ln_b"][j]))
        put(f"abpw2{j}", fm(inp["a_b_pw2"][j]))
    put("bbout", fm(inp["b_b_out"][0]))
    put("cqg", np.asarray(inp["c_q_g"][0], np.float32)[:, None])
    put("ckg", np.asarray(inp["c_k_g"][0], np.float32)[:, None])
    return pv


def build_att_bias(rpb):
    rpb = np.asarray(rpb, np.float32)
    out = np.full((NH, 128, N_BT, 512), NEG, np.float32)
    rows = L // GRID_W
    rs = np.clip(np.arange(rows) - 4, 0, rows - 8)
    cs = np.clip(np.arange(GRID_W) - 8, 0, GRID_W - 16)
    p = np.arange(128)
    n = np.arange(512)
    for g in (0, 1, 3):
        for ci, kc in enumerate(ATT_CHUNKS[g]):
            tile = ATT_TILE0[g] + ci
            kr = (2 * kc + p // 64)[:, None]
            kcol = (p % 64)[:, None]
            qr = (8 * g + n // 64)[None, :]
            qcol = (n % 64)[None, :]
            valid = (kr >= rs[qr]) & (kr < rs[qr] + 8) & (kcol >= cs[qcol]) & (kcol < cs[qcol] + 16)
            dr = np.clip(kr - qr + 7, 0, 14)
            dc = np.clip(kcol - qcol, -15, 15) + 15
            vals = rpb[:, dr, dc]
            out[:, :, tile, :] = np.where(valid[None], vals, np.float32(NEG))
    return out.reshape(NH, 128, N_BT * 512)


def build_dft():
    bf = ml_dtypes.bfloat16
    p = np.arange(128)
    cidx = (np.arange(2)[None, :, None] * 128 + p[:, None, None]) * np.arange(256)[None, None, :]
    ang = 2.0 * np.pi * (cidx % 256) / 256.0
    cs = np.zeros((2, 128, 2, 512), np.float64)
    for v, ln in enumerate((L, CL)):
        nrm = 1.0 / np.sqrt(ln * 256.0)
        cs[v, :, :, 0:256] = np.cos(ang) * nrm
        cs[v, :, :, 256:512] = np.sin(ang) * nrm
    lidx = (np.arange(16)[None, :, None] * 128 + p[:, None, None]) * np.arange(L)[None, None, :]
    angl = 2.0 * np.pi * (lidx % L) / float(L)
    dl = np.stack([np.cos(angl), -np.sin(angl)])
    cidx2 = (np.arange(2)[None, :, None] * 128 + p[:, None, None]) * np.arange(CL)[None, None, :]
    angc = 2.0 * np.pi * (cidx2 % CL) / float(CL)
    dc = np.stack([np.cos(angc), -np.sin(angc)])
    return cs.astype(np.float32).astype(bf), dl.astype(np.float32).astype(bf), dc.astype(np.float32).astype(bf)


def shared_inputs(inp):
    cs, dl, dc = build_dft()
    sh = {
        "pvec": build_pvec(inp),
        "ident": np.eye(128, dtype=np.float32),
        "dft_cs": cs, "dft_l": dl, "dft_c": dc,
        "att_bias": build_att_bias(inp["c_rpb"][0]),
    }
    for k in ("mod_w", "a_w_pw1", "a_w_pw2", "b_w_out", "c_w_qkv", "c_w_o", "f_w_up", "f_w_down"):
        sh[k] = np.ascontiguousarray(np.asarray(inp[k], np.float32))
    return sh


def core_inputs(inp, core, sh):
    b0 = core * NB
    m = dict(sh)
    m["x"] = np.ascontiguousarray(np.asarray(inp["x"][b0:b0 + NB], np.float32).reshape(NB * L, D))
    m["ctx"] = np.ascontiguousarray(np.asarray(inp["ctx"][b0:b0 + NB], np.float32).reshape(NB * CL, D))
    rows = np.concatenate([np.asarray(inp["c"][b0:b0 + NB], np.float32), np.asarray(inp["c_ctx"], np.float32)[None, :]], axis=0)
    m["csil"] = np.ascontiguousarray(rows.reshape(3, 16, 128).transpose(2, 1, 0))
    return m


_PROG = {}


def kernel(**inputs):
    if "p" not in _PROG:
        _PROG["p"] = Prog()
    prog = _PROG["p"]
    sh = shared_inputs(inputs)
    in_maps = [core_inputs(inputs, c, sh) for c in range(NCORES)]
    res = run_bass_kernel_spmd(prog.nc, in_maps, core_ids=list(range(NCORES)))
    outs = [np.asarray(r["out"], np.float32).reshape(NB, L, D) for r in res.results]
    return np.concatenate(outs, axis=0)
```
